# Optimizing a Trainium2 kernel written in Bass

```python
import jax, jax.numpy as jnp
from jax import lax
import numpy as np

D_MODEL = 1024
BATCH = 4
SEQ = 8192
DEPTH = 2

GRID_W = 64
CTX_LEN = 256
N_MIXERS = 2
RMS_EPS = 1e-6
GLA_HEADS = 4
GLA_DK = D_MODEL // 2
GLA_DV = D_MODEL
GLA_HEAD_K = GLA_DK // GLA_HEADS
GLA_HEAD_V = GLA_DV // GLA_HEADS
GLA_GATE_RANK = 16
GLA_GATE_NORM = 16.0
GLA_CHUNK = 64
GLA_IN = 2 * GLA_DK + 2 * GLA_DV + 2 * GLA_GATE_RANK
LRU_WIDTH = 1280
LRU_BLOCKS = 5
LRU_BLOCK_W = LRU_WIDTH // LRU_BLOCKS
LRU_C = 8.0
CONV_W = 4

kernel_name = "hybrid_gla_rglru_diffusion_trunk"


def rms_norm(x, g):
    xf = x.astype(jnp.float32)
    y = xf * lax.rsqrt(jnp.mean(xf * xf, axis=-1, keepdims=True) + RMS_EPS)
    return (y * g.astype(jnp.float32)).astype(x.dtype)


def modulate(h, shift, scale):
    return h * (1 + scale) + shift


def to_col_major(z, rows):
    b, t, d = z.shape
    return z.reshape(b, rows, GRID_W, d).transpose(0, 2, 1, 3).reshape(b, t, d)


def from_col_major(z, rows):
    b, t, d = z.shape
    return z.reshape(b, GRID_W, rows, d).transpose(0, 2, 1, 3).reshape(b, t, d)


def gla_scan(q, k, v, log_a, s0):
    b, t, h, _ = q.shape
    dv = v.shape[-1]
    n = t // GLA_CHUNK

    def chunks(z):
        return z.reshape(b, n, GLA_CHUNK, h, z.shape[-1]).transpose(1, 0, 3, 2, 4)

    causal = jnp.tril(jnp.ones((GLA_CHUNK, GLA_CHUNK), dtype=bool))[:, :, None]

    def step(state, inp):
        qc, kc, vc, gc = (z.astype(jnp.float32) for z in inp)
        cum = jnp.cumsum(gc, axis=2)
        o_inter = jnp.einsum('bhck,bhkv->bhcv', qc * jnp.exp(cum), state)
        diff = cum[:, :, :, None, :] - cum[:, :, None, :, :]
        decay = jnp.exp(jnp.where(causal, diff, -jnp.inf))
        scores = jnp.einsum('bhijk,bhjk->bhij', qc[:, :, :, None, :] * decay, kc)
        o_intra = jnp.einsum('bhij,bhjv->bhiv', scores, vc)
        last = cum[:, :, -1:, :]
        state = state * jnp.exp(last[:, :, 0, :, None]) + jnp.einsum(
            'bhck,bhcv->bhkv', kc * jnp.exp(last - cum), vc)
        return state, o_inter + o_intra

    s_final, out = lax.scan(step, s0, (chunks(q), chunks(k), chunks(v), chunks(log_a)))
    out = out.transpose(1, 0, 3, 2, 4).reshape(b, t, h, dv)
    return out.astype(q.dtype), s_final


def gla_mixer(h, w_in, wg_f, bg_f, wg_b, bg_b, norm_w, w_out, s0_f, s0_b, need_out):
    bsz, t, _ = h.shape
    proj = h @ w_in
    splits = [GLA_DK, 2 * GLA_DK, 2 * GLA_DK + GLA_DV, 2 * GLA_DK + 2 * GLA_DV,
              2 * GLA_DK + 2 * GLA_DV + GLA_GATE_RANK]
    q, k, v, g, lr_f, lr_b = jnp.split(proj, splits, axis=-1)
    q = q.reshape(bsz, t, GLA_HEADS, GLA_HEAD_K) * (GLA_HEAD_K ** -0.5)
    k = k.reshape(bsz, t, GLA_HEADS, GLA_HEAD_K)
    v = v.reshape(bsz, t, GLA_HEADS, GLA_HEAD_V)

    def log_gate(lr, w, bias):
        z = (lr @ w + bias).astype(jnp.float32)
        return (jax.nn.log_sigmoid(z) / GLA_GATE_NORM).reshape(bsz, t, GLA_HEADS, GLA_HEAD_K)

    o_f, s_f = gla_scan(q, k, v, log_gate(lr_f, wg_f, bg_f), s0_f)
    o_b_rev, s_b = gla_scan(jnp.flip(q, 1), jnp.flip(k, 1), jnp.flip(v, 1),
                            jnp.flip(log_gate(lr_b, wg_b, bg_b), 1), s0_b)
    if not need_out:
        return None, s_f, s_b
    o = rms_norm(o_f + jnp.flip(o_b_rev, 1), norm_w).reshape(bsz, t, GLA_DV)
    y = (o * jax.nn.silu(g)) @ w_out
    return y, s_f, s_b


def centred_dwconv(z, w, bias):
    left = CONV_W // 2
    y = lax.conv_general_dilated(z, w[:, None, :].astype(z.dtype), window_strides=(1,),
                                 padding=[(left, CONV_W - 1 - left)],
                                 dimension_numbers=('NWC', 'WIO', 'NWC'),
                                 feature_group_count=z.shape[-1])
    return y + bias


def rglru_scan(z, w_a, b_a, w_x, b_x, lam, h0):
    bsz, t, _ = z.shape
    zb = z.reshape(bsz, t, LRU_BLOCKS, LRU_BLOCK_W)
    r = jax.nn.sigmoid((jnp.einsum('btnd,nde->btne', zb, w_a).reshape(bsz, t, LRU_WIDTH) + b_a).astype(jnp.float32))
    i = jax.nn.sigmoid((jnp.einsum('btnd,nde->btne', zb, w_x).reshape(bsz, t, LRU_WIDTH) + b_x).astype(jnp.float32))
    log_a = -LRU_C * r * jax.nn.softplus(-lam.astype(jnp.float32))
    a = jnp.exp(log_a)
    u = jnp.sqrt(-jnp.expm1(2.0 * log_a)) * (i * z.astype(jnp.float32))
    u = u.at[:, 0].add(a[:, 0] * h0)

    def combine(lhs, rhs):
        a_l, u_l = lhs
        a_r, u_r = rhs
        return a_l * a_r, a_r * u_l + u_r

    _, hs = lax.associative_scan(combine, (a, u), axis=1)
    return hs, hs[:, -1]


def rglru_mixer(h, w_in, conv_w, conv_b, p_f, p_b, w_out, h0_f, h0_b, need_out):
    proj = h @ w_in
    z, g = jnp.split(proj, [LRU_WIDTH], axis=-1)
    z = centred_dwconv(z, conv_w, conv_b)
    hf, s_f = rglru_scan(z, *p_f, h0_f)
    hb_rev, s_b = rglru_scan(jnp.flip(z, 1), *p_b, h0_b)
    if not need_out:
        return None, s_f, s_b
    y = ((hf + jnp.flip(hb_rev, 1)).astype(h.dtype) * jax.nn.silu(g)) @ w_out
    return y, s_f, s_b


def setup_inputs(seed: int = 0) -> dict:
    key = jax.random.key(seed)
    ks = iter(jax.random.split(key, 40))
    n_gla = (DEPTH + 1) // 2
    n_lru = DEPTH // 2
    D = D_MODEL

    def nrm(shape, std):
        return jax.random.normal(next(ks), shape, jnp.float32) * std

    u = jax.random.uniform(next(ks), (2, n_lru, LRU_WIDTH), jnp.float32, minval=0.9, maxval=0.999)
    a0 = u ** (1.0 / LRU_C)
    lam = jnp.log(a0) - jnp.log1p(-a0)
    bw = LRU_BLOCK_W ** -0.5
    return {
        "x": nrm((BATCH, SEQ, D), 1.0),
        "c": nrm((BATCH, D), 1.0),
        "ctx": nrm((BATCH, CTX_LEN, D), 1.0),
        "c_ctx": nrm((D,), 1.0),
        "ada_w": nrm((DEPTH, D, 3 * D), 0.5 * D ** -0.5),
        "ada_b": nrm((DEPTH, 3 * D), 0.02),
        "norm_pre": 1.0 + nrm((DEPTH, D), 0.05),
        "norm_post": 1.0 + nrm((DEPTH, D), 0.05),
        "gla_w_in": nrm((n_gla, D, GLA_IN), D ** -0.5),
        "gla_wg_f": nrm((n_gla, GLA_GATE_RANK, GLA_DK), GLA_GATE_RANK ** -0.5),
        "gla_bg_f": nrm((n_gla, GLA_DK), 0.5),
        "gla_wg_b": nrm((n_gla, GLA_GATE_RANK, GLA_DK), GLA_GATE_RANK ** -0.5),
        "gla_bg_b": nrm((n_gla, GLA_DK), 0.5),
        "gla_norm": 1.0 + nrm((n_gla, GLA_HEAD_V), 0.05),
        "gla_w_out": nrm((n_gla, GLA_DV, D), GLA_DV ** -0.5),
        "lru_w_in": nrm((n_lru, D, 2 * LRU_WIDTH), D ** -0.5),
        "lru_conv_w": nrm((n_lru, CONV_W, LRU_WIDTH), CONV_W ** -0.5),
        "lru_conv_b": nrm((n_lru, LRU_WIDTH), 0.02),
        "lru_wa_f": nrm((n_lru, LRU_BLOCKS, LRU_BLOCK_W, LRU_BLOCK_W), bw),
        "lru_ba_f": nrm((n_lru, LRU_WIDTH), 0.1),
        "lru_wx_f": nrm((n_lru, LRU_BLOCKS, LRU_BLOCK_W, LRU_BLOCK_W), bw),
        "lru_bx_f": nrm((n_lru, LRU_WIDTH), 0.1),
        "lru_lam_f": lam[0],
        "lru_wa_b": nrm((n_lru, LRU_BLOCKS, LRU_BLOCK_W, LRU_BLOCK_W), bw),
        "lru_ba_b": nrm((n_lru, LRU_WIDTH), 0.1),
        "lru_wx_b": nrm((n_lru, LRU_BLOCKS, LRU_BLOCK_W, LRU_BLOCK_W), bw),
        "lru_bx_b": nrm((n_lru, LRU_WIDTH), 0.1),
        "lru_lam_b": lam[1],
        "lru_w_out": nrm((n_lru, LRU_WIDTH, D), LRU_WIDTH ** -0.5),
    }


def reference(x, c, ctx, c_ctx, ada_w, ada_b, norm_pre, norm_post,
              gla_w_in, gla_wg_f, gla_bg_f, gla_wg_b, gla_bg_b, gla_norm, gla_w_out,
              lru_w_in, lru_conv_w, lru_conv_b, lru_wa_f, lru_ba_f, lru_wx_f, lru_bx_f, lru_lam_f,
              lru_wa_b, lru_ba_b, lru_wx_b, lru_bx_b, lru_lam_b, lru_w_out):
    bsz, seq, _ = x.shape
    rows = seq // GRID_W
    sc = jax.nn.silu(c)
    sc_ctx = jax.nn.silu(c_ctx)
    for i in range(DEPTH):
        last = i == DEPTH - 1
        j = i // N_MIXERS
        mod = sc @ ada_w[i] + ada_b[i]
        mod_c = sc_ctx @ ada_w[i] + ada_b[i]
        sh, scl, gt = jnp.split(mod[:, None, :], 3, axis=-1)
        sh_c, scl_c, gt_c = jnp.split(mod_c, 3, axis=-1)
        h = modulate(rms_norm(x, norm_pre[i]), sh, scl)
        h_c = modulate(rms_norm(ctx, norm_pre[i]), sh_c, scl_c)
        if i % N_MIXERS == 0:
            params = (gla_w_in[j], gla_wg_f[j], gla_bg_f[j], gla_wg_b[j], gla_bg_b[j], gla_norm[j], gla_w_out[j])
            s0 = jnp.zeros((ctx.shape[0], GLA_HEADS, GLA_HEAD_K, GLA_HEAD_V), jnp.float32)
            y_c, s_f, s_b = gla_mixer(h_c, *params, s0, s0, not last)
            y, _, _ = gla_mixer(h, *params, s_f, s_b, True)
        else:
            p_f = (lru_wa_f[j], lru_ba_f[j], lru_wx_f[j], lru_bx_f[j], lru_lam_f[j])
            p_b = (lru_wa_b[j], lru_ba_b[j], lru_wx_b[j], lru_bx_b[j], lru_lam_b[j])
            h0 = jnp.zeros((ctx.shape[0], LRU_WIDTH), jnp.float32)
            y_c, s_f, s_b = rglru_mixer(h_c, lru_w_in[j], lru_conv_w[j], lru_conv_b[j], p_f, p_b,
                                        lru_w_out[j], h0, h0, not last)
            y, _, _ = rglru_mixer(to_col_major(h, rows), lru_w_in[j], lru_conv_w[j], lru_conv_b[j],
                                  p_f, p_b, lru_w_out[j], s_f, s_b, True)
            y = from_col_major(y, rows)
        x = x + gt * rms_norm(y, norm_post[i])
        if not last:
            ctx = ctx + gt_c * rms_norm(y_c, norm_post[i])
    return x
```

```python
import math
import os
import numpy as np
import ml_dtypes
import concourse.bass as bass
import concourse.mybir as mybir
from concourse.bass_utils import run_bass_kernel_spmd

F32 = mybir.dt.float32
BF16 = mybir.dt.bfloat16
AF = mybir.ActivationFunctionType
ALU = mybir.AluOpType
AX = mybir.AxisListType

D = 1024
T = 8192
CTX = 256
NB = 16
BLK = 512
EPS = 1e-6
GIN = 3104
LW = 1280
NCT = 10


class Buf:
    __slots__ = ("name", "w", "r")

    def __init__(self, name=""):
        self.name = name
        self.w = None
        self.r = []


class Sched:
    ENGS = ("pe", "dve", "act", "pool", "sp")
    RING = 8

    def __init__(self, nc, needed=None):
        self.nc = nc
        self.needed = needed
        self.used = set()
        self.rank = {}
        self.nsig = {e: 0 for e in self.ENGS}
        self.streams = {e: [] for e in self.ENGS}
        self.count = {e: 0 for e in self.ENGS}
        self.sems = {}
        for e in ("pe", "dve", "act", "pool"):
            self.sems[e] = nc.alloc_semaphore("s_" + e)
        self.dma_n = {}
        for q in ("sp", "act", "pool"):
            self.dma_n[q] = 0
            for k in range(self.RING):
                self.sems[("dma", q, k)] = nc.alloc_semaphore("d_%s%d" % (q, k))
        self.seen = {e: {} for e in self.ENGS}
        self.n_inst = {e: 0 for e in self.ENGS}

    def _need(self, eng, reads, writes):
        need = {}

        def add(tok):
            if tok is None:
                return
            k, v = tok
            if k == "pe" and eng == "pe":
                return
            if need.get(k, 0) < v:
                need[k] = v
        for b in reads:
            add(b.w)
        for b in writes:
            add(b.w)
            for t in b.r:
                add(t)
        out = []
        seen = self.seen[eng]
        for k, v in need.items():
            if seen.get(k, 0) < v:
                seen[k] = v
                out.append((self.sems[k], self._phys(k, v)))
        return out

    def _phys(self, k, v):
        if isinstance(k, tuple):
            return v
        self.used.add((k, v))
        if self.needed is None:
            return v
        return self.rank[(k, v)]

    def _mark(self, tok, reads, writes):
        for b in reads:
            b.r.append(tok)
            if len(b.r) > 24:
                m = {}
                for k, v in b.r:
                    if m.get(k, 0) < v:
                        m[k] = v
                b.r = list(m.items())
        for b in writes:
            b.w = tok
            b.r = []

    def op(self, eng, fn, reads=(), writes=(), signal=True):
        waits = self._need(eng, reads, writes)
        if signal:
            self.count[eng] += 1
            tok = (eng, self.count[eng])
            if self.needed is not None:
                if tok in self.needed:
                    self.nsig[eng] += 1
                    self.rank[tok] = self.nsig[eng]
                else:
                    signal = False
        else:
            tok = (eng, self.count[eng] + 1)
        sem = self.sems[eng]
        self.n_inst[eng] += 1

        def run(e, fn=fn, waits=waits, signal=signal, sem=sem):
            for s, v in waits:
                e.wait_ge(s, v)
            ins = fn(e)
            if signal:
                ins.then_inc(sem, 1)
        self.streams[eng].append(run)
        self._mark(tok, reads, writes)

    def dma(self, q, out_ap, in_ap, reads=(), writes=()):
        i = self.dma_n[q]
        self.dma_n[q] += 1
        k = ("dma", q, i % self.RING)
        base = 16 * (i // self.RING)
        waits = self._need(q, reads, writes)
        seen = self.seen[q]
        if base > 0 and seen.get(k, 0) < base:
            seen[k] = base
            waits.append((self.sems[k], base))
        tok = (k, base + 16)
        sem = self.sems[k]
        self.n_inst[q] += 1

        def run(e, waits=waits, sem=sem, out_ap=out_ap, in_ap=in_ap):
            for s, v in waits:
                e.wait_ge(s, v)
            e.dma_start(out=out_ap, in_=in_ap).then_inc(sem, 16)
        self.streams[q].append(run)
        self._mark(tok, reads, writes)

    def _all_tokens(self):
        toks = []
        for e in ("pe", "dve", "act", "pool"):
            if self.count[e] > 0:
                toks.append((e, self.count[e]))
        for q in ("sp", "act", "pool"):
            n = self.dma_n[q]
            for k in range(self.RING):
                nk = (n - k + self.RING - 1) // self.RING if n > k else 0
                if nk > 0:
                    toks.append((("dma", q, k), 16 * nk))
        return toks

    def barrier(self):
        toks = self._all_tokens()
        for eng in self.ENGS:
            waits = []
            seen = self.seen[eng]
            for k, v in toks:
                if seen.get(k, 0) < v:
                    seen[k] = v
                    waits.append((self.sems[k], self._phys(k, v)))

            def run(e, waits=waits):
                for s, v in waits:
                    e.wait_ge(s, v)
            self.streams[eng].append(run)

    def emit(self):
        nc = self.nc
        with nc.Block() as block:
            @block.tensor
            def _(e):
                for f in self.streams["pe"]:
                    f(e)

            @block.vector
            def _(e):
                for f in self.streams["dve"]:
                    f(e)

            @block.scalar
            def _(e):
                for f in self.streams["act"]:
                    f(e)

            @block.gpsimd
            def _(e):
                for f in self.streams["pool"]:
                    f(e)

            @block.sync
            def _(e):
                for f in self.streams["sp"]:
                    f(e)


class _Cut(Exception):
    pass


def _cut(n):
    if int(os.environ.get('LRU_CUT', '0')) == n:
        raise _Cut()


class Arena:
    def __init__(self, nc, name, nbytes):
        self.t = nc.alloc_sbuf_tensor(name, [128, nbytes // 4], F32)
        self.cap = nbytes
        self.off = 0

    def alloc(self, free_shape, dtype):
        n = 1
        for s in free_shape:
            n *= s
        sz = n * (2 if dtype == BF16 else 4)
        sz = (sz + 31) // 32 * 32
        assert self.off + sz <= self.cap, ("arena overflow", self.off, sz, self.cap)
        v = self.t[:, self.off // 4:(self.off + sz) // 4]
        if dtype == BF16:
            v = v.bitcast(BF16)
        v = v[:, 0:n]
        if len(free_shape) == 2:
            v = v.rearrange("p (a b) -> p a b", a=free_shape[0])
        elif len(free_shape) == 3:
            v = v.rearrange("p (a b c) -> p a b c", a=free_shape[0], b=free_shape[1])
        self.off += sz
        return v

    def mark(self):
        return self.off

    def reset(self, m):
        self.off = m


def build(layers=(0, 1)):
    _, used = _build(layers, None)
    nc, _ = _build(layers, used)
    return nc


def _build(layers, needed):
    nc = bass.Bass("TRN2", target_bir_lowering=False)
    S = Sched(nc, needed)
    do0 = 0 in layers
    do1 = 1 in layers

    def din(name, shape, dt=F32):
        return nc.dram_tensor(name, list(shape), dt, kind="ExternalInput").ap()

    def dscr(name, shape, dt=F32):
        return nc.dram_tensor(name, list(shape), dt, kind="Internal").ap()

    def dout(name, shape, dt=F32):
        return nc.dram_tensor(name, list(shape), dt, kind="ExternalOutput").ap()

    x_d = din("x", [T, D])
    ctx_d = din("ctx", [CTX, D])
    cvec_d = din("cvec", [128, 8, 2])
    ada_w_d = din("ada_w", [2, D, 3 * D])
    ada_bT_d = din("ada_bT", [128, 2, 24])
    ada_b_d = din("ada_b", [2, 3 * D])
    npreT_d = din("npreT", [128, 2, 8])
    npost_d = din("norm_post", [2, D])
    ident_d = din("ident", [128, 128])
    cmask_d = din("cmask", [128, 2, BLK])
    tmask_d = din("tmask", [128, 2, 4, 128])
    if do0:
        gwin_d = din("gla_w_in", [D, GIN])
        gwg_d = din("gla_wg", [2, 16, 512])
        gbgT_d = din("gla_bgT", [128, 2, 4])
        gnT_d = din("gla_normT", [128, 2])
        gwout_d = din("gla_w_out", [D, D])
    if do1:
        lwin_d = din("lru_w_in", [D, 2 * LW])
        lcwT_d = din("lru_cwT", [128, NCT, 4])
        lvT_d = din("lru_vT", [128, 7, NCT])
        lgate_d = din("lru_gates", [4, 5, 256, 256])
        lwout_d = din("lru_w_out", [LW, D])
    if do0 and do1:
        x1_d = dscr("x1", [T, D])
        ctx1_d = dscr("ctx1", [CTX, D])
    elif do0:
        x1_d = dout("x1", [T, D])
        ctx1_d = dout("ctx1", [CTX, D])
    else:
        x1_d = x_d
        ctx1_d = ctx_d
    if do1:
        out_d = dout("out", [T, D])

    bx1 = [Buf("x1_%d" % i) for i in range(NB + 1)]

    AR = Arena(nc, "arena", 211968)
    PS = nc.alloc_psum_tensor("psum", [128, 4096], F32)
    psb = [Buf("ps%d" % i) for i in range(8)]

    def bank(i, n=512):
        return PS[:, i * 512:i * 512 + n]

    def MM(out, lhsT, rhs, start, stop, r, w, sig=None):
        S.op("pe", lambda e: e.matmul(out, lhsT, rhs, start=start, stop=stop), r, w,
             signal=(stop if sig is None else sig))

    def TR(out, in_, ident, r, w, sig):
        S.op("pe", lambda e: e.transpose(out, in_, ident), r, w, signal=sig)

    def ACT(out, in_, func, r, w, bias=None, scale=None, accum=None):
        kw = {}
        if bias is not None:
            kw["bias"] = bias
        if scale is not None:
            kw["scale"] = scale
        if accum is not None:
            kw["accum_out"] = accum
        S.op("act", lambda e: e.activation(out, in_, func, **kw), r, w)

    def TT(eng, out, a, b, op, r, w):
        S.op(eng, lambda e: e.tensor_tensor(out, a, b, op), r, w)

    def TS(eng, out, a, s1, op0, r, w, s2=None, op1=None):
        if op1 is None:
            S.op(eng, lambda e: e.tensor_scalar(out, a, s1, None, op0), r, w)
        else:
            S.op(eng, lambda e: e.tensor_scalar(out, a, s1, s2, op0, op1), r, w)

    def STT(out, in0, scalar, in1, op0, op1, r, w):
        S.op("dve", lambda e: e.scalar_tensor_tensor(out, in0, scalar, in1, op0, op1), r, w)

    def CP(eng, out, in_, r, w):
        if eng == "act":
            S.op("act", lambda e: e.copy(out, in_), r, w)
        else:
            S.op(eng, lambda e: e.tensor_copy(out, in_), r, w)

    def SCAN(out, d0, d1, init, r, w):
        S.op("dve", lambda e: e.tensor_tensor_scan(out, d0, d1, init, ALU.mult, ALU.add), r, w)

    def MEMSET(eng, ap, val, w):
        S.op(eng, lambda e: e.memset(ap, val), (), w)

    def RECIP(out, in_, r, w):
        S.op("dve", lambda e: e.reciprocal(out, in_), r, w)

    ID32 = AR.alloc([128], F32); bID32 = Buf()
    ID16 = AR.alloc([128], BF16); bID16 = Buf()
    ONES16 = AR.alloc([128], BF16); bONES = Buf()
    TM = AR.alloc([2, 4, 128], BF16); bTM = Buf()
    CV = AR.alloc([8, 2], F32); bCV = Buf()
    MODT = AR.alloc([2, 24, 2], F32); bMODT = Buf()
    ABT = AR.alloc([2, 24], F32); bABT = Buf()
    NPT = AR.alloc([2, 8], F32); bNPT = Buf()
    GV = AR.alloc([2, 2, 8], F32); bGV = Buf()
    gpl_s = [[dscr("gpl_s%d%d" % (i, v), [128, D]) for v in range(2)] for i in range(2)]
    bgpl_s = [[Buf() for v in range(2)] for i in range(2)]
    SS = AR.alloc([8], F32); bSS = Buf()
    RS = AR.alloc([8], F32); bRS = Buf()
    SSY = AR.alloc([4], F32); bSSY = Buf()
    MHALF = AR.alloc([8], F32); bMH = Buf()

    S.dma("sp", ID32, ident_d, writes=[bID32])
    S.dma("sp", CV, cvec_d, writes=[bCV])
    S.dma("sp", ABT, ada_bT_d, writes=[bABT])
    S.dma("sp", NPT, npreT_d, writes=[bNPT])
    CP("dve", ID16, ID32, [bID32], [bID16])
    MEMSET("dve", ONES16, 1.0, [bONES])
    MEMSET("dve", MHALF, -0.5, [bMH])
    ACT(CV, CV, AF.Silu, [bCV], [bCV])

    def run_prologue():
        mk0 = AR.mark()
        TM32 = AR.alloc([2, 4, 128], F32); bTM32 = Buf()
        S.dma("sp", TM32, tmask_d, writes=[bTM32])
        CP("dve", TM, TM32, [bTM32], [bTM])
        SEL = AR.alloc([2, 128], F32); bSEL = Buf()
        for v in range(2):
            CP("dve", SEL[0:2, v, :], ID32[0:2, v:v + 1].broadcast_to([2, 128]), [bID32], [bSEL])
        AWs = [AR.alloc([8, 512], F32) for _ in range(2)]; bAWs = [Buf(), Buf()]
        GPL = [[AR.alloc([D], F32) for v in range(2)] for i in range(2)]
        bGPL = [[Buf() for v in range(2)] for i in range(2)]
        MR = AR.alloc([3 * D], F32); bMR = Buf()
        ABROW = AR.alloc([3 * D], F32); bABROW = Buf()
        NPR = AR.alloc([D], F32); bNPR = Buf()
        nch = 0
        for li in layers:
            S.dma("sp", ABROW[0:2, :], ada_b_d[li:li + 1, :].broadcast_to([2, 3 * D]), writes=[bABROW])
            S.dma("sp", NPR, npost_d[li:li + 1, :].broadcast_to([128, D]), writes=[bNPR])
            for ch in range(6):
                AW = AWs[nch % 2]; bAW = bAWs[nch % 2]
                nch += 1
                S.dma("sp", AW, ada_w_d[li, :, ch * 512:(ch + 1) * 512].rearrange("(dc p) n -> p dc n", p=128),
                      writes=[bAW])
                pb = ch % 2
                for dc in range(8):
                    MM(bank(pb)[0:2, :], CV[:, dc, :], AW[:, dc, :], dc == 0, dc == 7, [bAW, bCV], [psb[pb]])
                TT("dve", MR[0:2, ch * 512:(ch + 1) * 512], bank(pb)[0:2, :], ABROW[0:2, ch * 512:(ch + 1) * 512], ALU.add,
                   [psb[pb], bABROW], [bMR])
            for j in range(16):
                TR(bank(2)[:, 2 * j:2 * j + 2], MR[0:2, j * 128:(j + 1) * 128], ID32[0:2, 0:2], [bMR, bID32], [psb[2]], j == 15)
            CP("dve", MODT[:, li, 0:16, :], bank(2)[:, 0:32].rearrange("p (a b) -> p a b", b=2), [psb[2]], [bMODT])
            for v in range(2):
                for hf in range(2):
                    pg = bank(4 + hf)
                    MM(pg, SEL[0:2, v, :], MR[0:2, 2 * D + hf * 512:2 * D + (hf + 1) * 512], True, True, [bSEL, bMR], [psb[4 + hf]])
                    cs = slice(hf * 512, (hf + 1) * 512)
                    TT("dve", GPL[li][v][:, cs], pg, NPR[:, cs], ALU.mult, [psb[4 + hf], bNPR], [bGPL[li][v]])
                STT(GV[:, li, v, :], MODT[:, li, 8:16, v], 1.0, NPT[:, li, :], ALU.add, ALU.mult,
                    [bMODT, bNPT], [bGV])
                S.dma("sp", gpl_s[li][v], GPL[li][v], reads=[bGPL[li][v]], writes=[bgpl_s[li][v]])
        S.barrier()
        AR.reset(mk0)

    if not do0:
        run_prologue()

    def front_end(XT, bXT, HT, bHT, nt, li, v, JK, bJK, pa_banks):
        N = nt * 128
        for tt in range(nt):
            ACT(JK, XT[:, tt, :], AF.Square, [bXT], [bJK, bSS], accum=SS[:, tt:tt + 1])
        if li == 0:
            ACT(RS[:, 0:nt], SS[:, 0:nt], AF.Ln, [bSS], [bRS], bias=EPS, scale=1.0 / D)
            ACT(RS[:, 0:nt], RS[:, 0:nt], AF.Exp, [bRS], [bRS], scale=-0.5)
        else:
            TS("pool", RS[:, 0:nt], SS[:, 0:nt], 1.0 / D, ALU.mult, [bSS], [bRS], s2=EPS, op1=ALU.add)
            TT("pool", RS[:, 0:nt], RS[:, 0:nt], MHALF[:, 0:nt], ALU.pow, [bRS, bMH], [bRS])
        for tt in range(nt):
            TS("dve", XT[:, tt, :], XT[:, tt, :], RS[:, tt:tt + 1], ALU.mult, [bXT, bRS], [bXT])
        for dc in range(8):
            pb = pa_banks[dc % 2]
            for tt in range(nt):
                TR(bank(pb)[:, tt * 128:(tt + 1) * 128], XT[:, tt, dc * 128:(dc + 1) * 128], ID32,
                   [bXT, bID32], [psb[pb]], tt == nt - 1)
            ACT(HT[:, dc, 0:N], bank(pb)[:, 0:N], AF.Identity, [psb[pb], bGV, bMODT], [bHT],
                bias=MODT[:, li, dc, v:v + 1], scale=GV[:, li, v, dc:dc + 1])

    def back_end(py, pyb, XTt, bXT, GP, bGP, TMPY, bTMPY, JK, bJK, li=0):
        ACT(JK, py, AF.Square, pyb, [bJK, bSSY], accum=SSY[:, 0:1])
        if li == 0:
            ACT(SSY[:, 1:2], SSY[:, 0:1], AF.Ln, [bSSY], [bSSY], bias=EPS, scale=1.0 / D)
            ACT(SSY[:, 2:3], SSY[:, 1:2], AF.Exp, [bSSY], [bSSY], scale=-0.5)
        else:
            TS("pool", SSY[:, 1:2], SSY[:, 0:1], 1.0 / D, ALU.mult, [bSSY], [bSSY], s2=EPS, op1=ALU.add)
            TT("pool", SSY[:, 2:3], SSY[:, 1:2], MHALF[:, 0:1], ALU.pow, [bSSY, bMH], [bSSY])
        STT(TMPY, py, SSY[:, 2:3], GP, ALU.mult, ALU.mult, pyb + [bSSY, bGP], [bTMPY])
        TT("dve", XTt, TMPY, XTt, ALU.add, [bTMPY, bXT], [bXT])

    if do0:
        NBLK = NB + 1
        qtb_s = dscr("qtb_s", [NBLK, 128, 4 * BLK], BF16)
        ktb_s = dscr("ktb_s", [NBLK, 128, 4 * BLK], BF16)
        ktok_s = dscr("ktok_s", [NBLK, 128, 4 * 512], BF16)
        vv_s = dscr("vv_s", [NBLK, 128, 4 * D], BF16)
        sg_s = dscr("sg_s", [NBLK, 128, 8 * BLK], BF16)
        of_s = dscr("of_s", [NBLK, 128, 4 * 1024], F32)
        bsc = [[Buf() for _ in range(NBLK)] for _ in range(6)]

        L0 = AR.mark()
        WOUT = AR.alloc([8, D], BF16); bWOUT = Buf()
        CM = AR.alloc([2, BLK], F32); bCM = Buf()
        S.dma("sp", CM, cmask_d, writes=[bCM])
        WG = AR.alloc([2, 512], BF16); bWG = Buf()
        BGN = AR.alloc([2, 4], F32); bBGN = Buf()
        GN = AR.alloc([2], F32); bGN = Buf()
        ELB = AR.alloc([4, 4 * NBLK], F32); bELB = Buf()
        S32 = AR.alloc([4, 256], F32); bS32 = Buf()
        S16 = AR.alloc([4, 256], BF16); bS16 = Buf()
        T1 = AR.alloc([4, 256], F32); bT1 = Buf()
        SCM = AR.alloc([4, 128], BF16); bSCM = Buf()
        JK = AR.alloc([D], BF16); bJK = Buf()
        GPL = [[AR.alloc([D], F32) for v in range(2)], None]
        bGPL = [[Buf(), Buf()], None]
        for vc in range(8):
            S.dma("pool", WOUT[:, vc, :], gwout_d[vc * 128:(vc + 1) * 128, :], writes=[bWOUT])
        for d in range(2):
            S.dma("pool", WG[0:16, d, :], gwg_d[d], writes=[bWG])
        S.dma("sp", BGN, gbgT_d, writes=[bBGN])
        S.dma("sp", GN, gnT_d, writes=[bGN])
        TS("dve", BGN, BGN, -1.0, ALU.mult, [bBGN], [bBGN])
        SCM2 = AR.alloc([4, 128], BF16)
        L0P = AR.mark()
        WQ = AR.alloc([8, GIN], BF16); bWQ = Buf()
        for dc in range(8):
            S.dma("pool", WQ[:, dc, :], gwin_d[dc * 128:(dc + 1) * 128, :], writes=[bWQ])
        run_prologue()
        for v in range(2):
            S.dma("sp", GPL[0][v], gpl_s[0][v], reads=[bgpl_s[0][v]], writes=[bGPL[0][v]])

        def blk_src(bi):
            if bi == 0:
                return ctx_d.rearrange("(tt p) d -> p tt d", p=128), 2
            s = (bi - 1) * BLK
            return x_d[s:s + BLK, :].rearrange("(tt p) d -> p tt d", p=128), 4

        SCMs = [SCM, SCM2]; bSCMs = [bSCM, Buf()]

        def sc_pre(ci, par, QT, KT, KTOK, V, bins, mask_i):
            for hd in range(4):
                MM(bank(2)[:, hd * 128:(hd + 1) * 128], KT[:, hd, ci * 128:(ci + 1) * 128],
                   QT[:, hd, ci * 128:(ci + 1) * 128], True, True, bins, [psb[2]], sig=(hd == 3))
            TT("dve", SCMs[par], bank(2).rearrange("p (a b) -> p a b", a=4), TM[:, mask_i], ALU.mult,
               [psb[2], bTM], [bSCMs[par]])
            for hd in range(4):
                MM(PS[:, 5 * 512 + hd * 256: 5 * 512 + (hd + 1) * 256], KTOK[:, ci, hd * 128:(hd + 1) * 128],
                   V[:, ci, hd * 256:(hd + 1) * 256], True, True, bins, [psb[5 + hd // 2]], sig=(hd % 2 == 1))

        def sc_out(ci, par, QT, V, bins, pot_out):
            for hd in range(4):
                for vv in range(2):
                    o = PS[:, 3 * 512 + (hd * 2 + vv) * 128: 3 * 512 + (hd * 2 + vv + 1) * 128]
                    pb = psb[3 + (hd // 2)]
                    MM(o, S16[:, hd, vv * 128:(vv + 1) * 128], QT[:, hd, ci * 128:(ci + 1) * 128],
                       True, False, bins + [bS16], [pb])
                    MM(o, V[:, ci, hd * 256 + vv * 128: hd * 256 + (vv + 1) * 128], SCMs[par][:, hd, :],
                       False, True, bins + [bSCMs[par]], [pb], sig=(vv == 1 and hd % 2 == 1))
            pot_out()

        def sc_upd(EL, bEL, elcol):
            TT("dve", T1, PS[:, 5 * 512:7 * 512].rearrange("p (a b) -> p a b", a=4), S32, ALU.add,
               [psb[5], psb[6], bS32], [bT1])
            TT("dve", S32, T1, EL[:, :, elcol:elcol + 1].broadcast_to([128, 4, 256]), ALU.mult,
               [bT1, bEL], [bS32])
            CP("act", S16, S32, [bS32], [bS16])

        def scan_thunks(cis, QT, KT, KTOK, V, bins, mask_i, EL, bEL, elcol_of, pot_of):
            th = []
            n = len(cis)
            th.append(lambda: sc_pre(cis[0], 0, QT, KT, KTOK, V, bins, mask_i))
            for i, ci in enumerate(cis):
                th.append(lambda i=i, ci=ci: sc_out(ci, i % 2, QT, V, bins, pot_of(ci)))
                th.append(lambda i=i, ci=ci: sc_upd(EL, bEL, elcol_of(ci)))
                if i + 1 < n:
                    th.append(lambda i=i: sc_pre(cis[i + 1], (i + 1) % 2, QT, KT, KTOK, V, bins, mask_i))
            return th

        XT = AR.alloc([4, D], F32); bXT = Buf()
        HT = AR.alloc([8, BLK], BF16); bHT = Buf()
        LR16 = AR.alloc([2, BLK], BF16); bLR = Buf()
        EB = [AR.alloc([2, BLK], F32) for _ in range(2)]; bEB = [Buf(), Buf()]
        CB = [AR.alloc([2, BLK], F32) for _ in range(2)]; bCB = [Buf(), Buf()]
        QTb = AR.alloc([4, BLK], BF16); bQTb = Buf()
        KTb = AR.alloc([4, BLK], BF16); bKTb = Buf()
        KTOKb = AR.alloc([4, 512], BF16); bKTOKb = Buf()
        QTf = [AR.alloc([4, BLK], BF16) for _ in range(2)]; bQTf = [Buf(), Buf()]
        KTf = [AR.alloc([4, BLK], BF16) for _ in range(2)]; bKTf = [Buf(), Buf()]
        KTOKf = [AR.alloc([4, 512], BF16) for _ in range(2)]; bKTOKf = [Buf(), Buf()]
        VVs = [AR.alloc([4, D], BF16) for _ in range(2)]; bVVs = [Buf(), Buf()]
        ELFs = [AR.alloc([4, 4], F32) for _ in range(2)]; bELFs = [Buf(), Buf()]
        SGt = [AR.alloc([BLK], BF16) for _ in range(2)]; bSGt = [Buf(), Buf()]
        OT = AR.alloc([8, 128], F32); bOT = Buf()
        PKT = bank(7).bitcast(BF16)
        lnscale = math.log(128.0 ** -0.5)

        def merge2(*lists):
            lists = [l for l in lists if l]
            items = []
            for li_, l in enumerate(lists):
                for i, t in enumerate(l):
                    items.append(((i + 0.5) / len(l), li_, i, t))
            items.sort(key=lambda x: (x[0], x[1], x[2]))
            return [t for _, _, _, t in items]

        def P_thunks(bi):
            th = []
            sl = bi % 2
            src, nt = blk_src(bi)
            N = nt * 128
            v = 1 if bi == 0 else 0
            QT = (QTf[sl], QTb); bQT = (bQTf[sl], bQTb)
            KT = (KTf[sl], KTb); bKT = (bKTf[sl], bKTb)
            KTOK = (KTOKf[sl], KTOKb); bKTOK = (bKTOKf[sl], bKTOKb)
            VV = VVs[sl]; bVV = bVVs[sl]
            ELF = ELFs[sl]; bELF = bELFs[sl]

            def f_front():
                front_end(XT, bXT, HT, bHT, nt, 0, v, JK, bJK, (0, 1))
                if bi + 1 < NBLK:
                    srcn, ntn = blk_src(bi + 1)
                    S.dma("sp", XT[:, 0:ntn, :], srcn, writes=[bXT])
            th.append(f_front)

            def f_lr():
                for d in range(2):
                    for dc in range(8):
                        MM(bank(d)[0:16, 0:N], WQ[:, dc, 3072 + 16 * d:3088 + 16 * d], HT[:, dc, 0:N], dc == 0, dc == 7,
                           [bWQ, bHT], [psb[d]])
                    CP("dve", LR16[0:16, d, 0:N], bank(d)[0:16, 0:N], [psb[d]], [bLR])
            th.append(f_lr)
            th_head = th
            th = []
            for hp in range(2):
                for d in range(2):
                    def f_gate(hp=hp, d=d):
                        for h2 in range(2):
                            hd = 2 * hp + h2
                            pb = h2
                            MM(bank(pb)[:, 0:N], WG[0:16, d, hd * 128:(hd + 1) * 128], LR16[0:16, d, 0:N], True, True,
                               [bWG, bLR], [psb[pb]])
                            ACT(EB[d][:, h2, 0:N], bank(pb)[:, 0:N], AF.Exp, [psb[pb], bBGN], [bEB[d]],
                                bias=BGN[:, d, hd:hd + 1], scale=-1.0)
                        ACT(EB[d][:, :, 0:N], EB[d][:, :, 0:N], AF.Ln, [bEB[d]], [bEB[d]], bias=1.0)
                        for h2 in range(2):
                            if d == 0:
                                SCAN(CB[d][:, h2, 0:N], CM[:, 0, 0:N], EB[d][:, h2, 0:N], 0.0, [bCM, bEB[d]], [bCB[d]])
                            else:
                                SCAN(CB[d][:, h2, 0:N][:, ::-1], CM[:, 1, 0:N][:, ::-1], EB[d][:, h2, 0:N][:, ::-1], 0.0,
                                     [bCM, bEB[d]], [bCB[d]])
                        if d == 0:
                            ACT(ELF[:, 2 * hp:2 * hp + 2, 0:nt], CB[d][:, :, 127:N:128], AF.Exp, [bCB[d]], [bELF], scale=-1.0 / 16)
                        else:
                            ACT(ELB[:, 2 * hp:2 * hp + 2, 4 * bi:4 * bi + nt], CB[d][:, :, 0:N:128], AF.Exp, [bCB[d]], [bELB],
                                scale=-1.0 / 16)
                        ACT(EB[d][:, :, 0:N], CB[d][:, :, 0:N], AF.Exp, [bCB[d]], [bEB[d]], bias=lnscale, scale=-1.0 / 16)
                        ACT(CB[d][:, :, 0:N], CB[d][:, :, 0:N], AF.Exp, [bCB[d]], [bCB[d]], scale=1.0 / 16)
                    th.append(f_gate)
                for isk in range(2):
                    for h2 in range(2):
                        def f_qk(hp=hp, isk=isk, h2=h2):
                            hd = 2 * hp + h2
                            pb = h2
                            for dc in range(8):
                                MM(bank(pb)[:, 0:N], WQ[:, dc, 512 * isk + hd * 128:512 * isk + (hd + 1) * 128], HT[:, dc, 0:N],
                                   dc == 0, dc == 7, [bWQ, bHT], [psb[pb]])
                            for d in range(2):
                                if isk == 0:
                                    TT("dve", QT[d][:, hd, 0:N], bank(pb)[:, 0:N], EB[d][:, h2, 0:N], ALU.mult,
                                       [psb[pb], bEB[d]], [bQT[d]])
                                else:
                                    TT("dve", KT[d][:, hd, 0:N], bank(pb)[:, 0:N], CB[d][:, h2, 0:N], ALU.mult,
                                       [psb[pb], bCB[d]], [bKT[d]])
                        th.append(f_qk)
            th_chain = th
            th = []
            for vc in range(8):
                def f_g(vc=vc):
                    pb = vc % 2
                    for dc in range(8):
                        MM(bank(pb)[:, 0:N], WQ[:, dc, 2048 + vc * 128:2048 + (vc + 1) * 128], HT[:, dc, 0:N], dc == 0, dc == 7,
                           [bWQ, bHT], [psb[pb]])
                    ACT(SGt[vc % 2][:, 0:N], bank(pb)[:, 0:N], AF.Silu, [psb[pb]], [bSGt[vc % 2]])
                    S.dma("sp", sg_s[bi][:, vc * BLK:vc * BLK + N], SGt[vc % 2][:, 0:N], reads=[bSGt[vc % 2]], writes=[bsc[4][bi]])
                th.append(f_g)
            for tt in range(nt):
                for hf in range(2):
                    def f_v(tt=tt, hf=hf):
                        pb = hf
                        for dc in range(8):
                            MM(bank(pb), HT[:, dc, tt * 128:(tt + 1) * 128], WQ[:, dc, 1024 + hf * 512:1024 + (hf + 1) * 512],
                               dc == 0, dc == 7, [bWQ, bHT], [psb[pb]])
                        CP("act", VV[:, tt, hf * 512:(hf + 1) * 512], bank(pb), [psb[pb]], [bVV])
                    th.append(f_v)
            th_proj = th
            th = th_head + th_chain + th_proj
            for d in range(2):
                for tt in range(nt):
                    def f_kt(d=d, tt=tt):
                        for hd in range(4):
                            TR(PKT[:, hd * 128:(hd + 1) * 128], KT[d][:, hd, tt * 128:(tt + 1) * 128], ID16,
                               [bKT[d], bID16], [psb[7]], hd == 3)
                        CP("dve", KTOK[d][:, tt, :], PKT[:, 0:512], [psb[7]], [bKTOK[d]])
                    th.append(f_kt)

            def f_store():
                S.dma("sp", qtb_s[bi].rearrange("p (a b) -> p a b", a=4)[:, :, 0:N], QTb[:, :, 0:N], reads=[bQTb], writes=[bsc[0][bi]])
                S.dma("sp", ktb_s[bi].rearrange("p (a b) -> p a b", a=4)[:, :, 0:N], KTb[:, :, 0:N], reads=[bKTb], writes=[bsc[1][bi]])
                S.dma("sp", ktok_s[bi].rearrange("p (a b) -> p a b", a=4)[:, 0:nt, :], KTOKb[:, 0:nt, :], reads=[bKTOKb], writes=[bsc[2][bi]])
                S.dma("sp", vv_s[bi].rearrange("p (a b) -> p a b", a=4)[:, 0:nt, :], VV[:, 0:nt, :], reads=[bVV], writes=[bsc[3][bi]])
            th.append(f_store)
            return th

        def Q_thunks(bi):
            th = []
            sl = bi % 2
            src, nt = blk_src(bi)
            def pot_of(ci):
                def pot_out():
                    CP("act", OT, PS[:, 3 * 512:5 * 512].rearrange("p (a b) -> p a b", a=8), [psb[3], psb[4]], [bOT])
                    S.dma("sp", of_s[bi][:, ci * 1024:(ci + 1) * 1024], OT.rearrange("p a b -> p (a b)"),
                          reads=[bOT], writes=[bsc[5][bi]])
                return pot_out
            return scan_thunks(list(range(nt)), QTf[sl], KTf[sl], KTOKf[sl], VVs[sl],
                               [bQTf[sl], bKTf[sl], bKTOKf[sl], bVVs[sl]], 0, ELFs[sl], bELFs[sl], lambda ci: ci, pot_of)

        MEMSET("dve", S32, 0.0, [bS32])
        MEMSET("pool", S16, 0.0, [bS16])
        src0, nt0 = blk_src(0)
        S.dma("sp", XT[:, 0:nt0, :], src0, writes=[bXT])
        for t_ in P_thunks(0):
            t_()
        for bi in range(NBLK):
            pn = P_thunks(bi + 1) if bi + 1 < NBLK else []
            for t_ in merge2(pn, Q_thunks(bi)):
                t_()

        S.barrier()
        AR.reset(L0P)

        LQ = [AR.alloc([4, BLK], BF16) for _ in range(2)]
        LK = [AR.alloc([4, BLK], BF16) for _ in range(2)]
        LKT = [AR.alloc([4, 512], BF16) for _ in range(2)]
        LV = [AR.alloc([4, D], BF16) for _ in range(2)]
        LSG = [AR.alloc([8, BLK], BF16) for _ in range(2)]
        LOF = [AR.alloc([4, 8, 128], F32) for _ in range(2)]
        LX = [AR.alloc([4, D], F32) for _ in range(2)]
        bL = [[Buf() for _ in range(7)] for _ in range(2)]
        bLOFc = [[Buf() for _ in range(4)] for _ in range(2)]
        OSQ = AR.alloc([4, 8, 128], BF16); bOSQ = Buf()
        RSTD = AR.alloc([4, 4, 128], F32); bRSTD = Buf()
        OG = AR.alloc([4, 8, 128], BF16); bOG = Buf()
        TMPY = AR.alloc([D], F32); bTMPY = Buf()

        def loadA(bi, slot):
            src, nt = blk_src(bi)
            N = nt * 128
            S.dma("sp", LQ[slot][:, :, 0:N], qtb_s[bi].rearrange("p (a b) -> p a b", a=4)[:, :, 0:N], reads=[bsc[0][bi]], writes=[bL[slot][0]])
            S.dma("sp", LK[slot][:, :, 0:N], ktb_s[bi].rearrange("p (a b) -> p a b", a=4)[:, :, 0:N], reads=[bsc[1][bi]], writes=[bL[slot][1]])
            S.dma("sp", LKT[slot][:, 0:nt, :], ktok_s[bi].rearrange("p (a b) -> p a b", a=4)[:, 0:nt, :], reads=[bsc[2][bi]], writes=[bL[slot][2]])
            S.dma("sp", LV[slot][:, 0:nt, :], vv_s[bi].rearrange("p (a b) -> p a b", a=4)[:, 0:nt, :], reads=[bsc[3][bi]], writes=[bL[slot][3]])

        def loadL(bi, slot):
            src, nt = blk_src(bi)
            N = nt * 128
            for ci in range(nt - 1, -1, -1):
                S.dma("sp", LOF[slot][:, ci].rearrange("p a b -> p (a b)"), of_s[bi][:, ci * 1024:(ci + 1) * 1024],
                      reads=[bsc[5][bi]], writes=[bLOFc[slot][ci]])
            S.dma("sp", LSG[slot][:, :, 0:N], sg_s[bi].rearrange("p (a b) -> p a b", a=8)[:, :, 0:N], reads=[bsc[4][bi]], writes=[bL[slot][4]])
            S.dma("sp", LX[slot][:, 0:nt, :], src, writes=[bL[slot][6]])

        def A_thunks(bi, slot):
            th = []
            src, nt = blk_src(bi)
            bq, bk, bkt, bv = bL[slot][0:4]
            def pot_of(ci):
                def pot_out():
                    TT("dve", LOF[slot][:, ci], PS[:, 3 * 512:5 * 512].rearrange("p (a b) -> p a b", a=8),
                       LOF[slot][:, ci], ALU.add, [psb[3], psb[4], bLOFc[slot][ci]], [bLOFc[slot][ci]])
                return pot_out
            return scan_thunks(list(range(nt - 1, -1, -1)), LQ[slot], LK[slot], LKT[slot], LV[slot], [bq, bk, bkt, bv], 1,
                               ELB, bELB, lambda ci: 4 * bi + ci, pot_of)

        def B_thunks(bi, slot):
            th = []
            src, nt = blk_src(bi)
            N = nt * 128
            v = 1 if bi == 0 else 0
            bsg = bL[slot][4]; bx = bL[slot][6]
            bofs = bLOFc[slot][0:nt]
            l_norm = []; l_scale = []; l_out = []
            for ci in range(nt):
                def f_norm(ci=ci):
                    ACT(OSQ[:, ci], LOF[slot][:, ci], AF.Square, [bLOFc[slot][ci]], [bOSQ])
                    for hd in range(4):
                        for vv in range(2):
                            MM(bank(7)[:, hd * 128:(hd + 1) * 128], ONES16, OSQ[:, ci, hd * 2 + vv, :], vv == 0, vv == 1,
                               [bONES, bOSQ], [psb[7]], sig=(vv == 1 and hd == 3))
                    ACT(RSTD[:, ci], bank(7).rearrange("p (a b) -> p a b", a=4), AF.Ln, [psb[7]], [bRSTD], bias=EPS, scale=1.0 / 256)
                    ACT(RSTD[:, ci], RSTD[:, ci], AF.Exp, [bRSTD], [bRSTD], scale=-0.5)
                l_norm.append(f_norm)

            for ci in range(nt):
                def f_scale(ci=ci):
                    O5 = LOF[slot].rearrange("p c (h v) t -> p c h v t", v=2)
                    for vv in range(2):
                        STT(O5[:, ci, :, vv, :], O5[:, ci, :, vv, :], GN[:, vv:vv + 1], RSTD[:, ci], ALU.mult, ALU.mult,
                            [bLOFc[slot][ci], bGN, bRSTD], [bLOFc[slot][ci]])
                    TT("dve", OG[:, ci], LOF[slot][:, ci], LSG[slot][:, :, ci * 128:(ci + 1) * 128], ALU.mult,
                       [bLOFc[slot][ci], bsg], [bOG])
                l_scale.append(f_scale)
            for ci in range(nt):
                def f_out(ci=ci):
                    for hf in range(2):
                        for vc in range(8):
                            MM(bank(hf), OG[:, ci, vc, :], WOUT[:, vc, hf * 512:(hf + 1) * 512], vc == 0, vc == 7,
                               [bOG, bWOUT], [psb[hf]])
                    back_end(PS[:, 0:1024], [psb[0], psb[1]], LX[slot][:, ci, :], bx, GPL[0][v], bGPL[0][v], TMPY, bTMPY, JK, bJK)
                l_out.append(f_out)
            for ci in range(nt):
                th.append(l_norm[ci])
                if ci >= 1:
                    th.append(l_scale[ci - 1])
                    th.append(l_out[ci - 1])
            th.append(l_scale[nt - 1])
            th.append(l_out[nt - 1])

            def f_store():
                if bi == 0:
                    S.dma("sp", ctx1_d.rearrange("(tt p) d -> p tt d", p=128), LX[slot][:, 0:2, :], reads=[bx], writes=[bx1[0]])
                else:
                    s_ = (bi - 1) * BLK
                    S.dma("sp", x1_d[s_:s_ + BLK, :].rearrange("(tt p) d -> p tt d", p=128), LX[slot][:, 0:4, :], reads=[bx], writes=[bx1[bi]])
            th.append(f_store)
            return th

        MEMSET("dve", S32, 0.0, [bS32])
        MEMSET("pool", S16, 0.0, [bS16])
        orderB = [0] + list(range(NBLK - 1, 0, -1))
        loadA(orderB[0], 0)
        loadL(orderB[0], 0)
        nO = len(orderB)
        for oi in range(nO + 1):
            ath = A_thunks(orderB[oi], oi % 2) if oi < nO else []
            bth = B_thunks(orderB[oi - 1], (oi - 1) % 2) if oi >= 1 else []
            if oi + 1 < nO:
                loadA(orderB[oi + 1], (oi + 1) % 2)
            if oi == 0 and nO > 1:
                loadL(orderB[1], 1)
            for t_ in merge2(ath, bth):
                t_()
            if oi >= 1 and oi + 1 < nO:
                loadL(orderB[oi + 1], (oi + 1) % 2)
        S.barrier()
        AR.reset(L0)

    def _lru():
        zc_s = dscr("zc_s", [NB, 3, 128, 4 * BLK], F32)
        hf_s = dscr("hf_s", [NB, NCT, 128, BLK], F32)
        sgl_s = dscr("sgl_s", [NB, 3, 128, 4 * BLK], BF16)
        zcx_s = dscr("zcx_s", [128, NCT * CTX], F32); bzcx_s = Buf()
        bzs = [[[Buf() for _ in range(NCT)] for _ in range(NB)] for _ in range(3)]
        x1c = x1_d.rearrange("(r c) d -> r c d", c=64)
        outc = out_d.rearrange("(r c) d -> r c d", c=64)
        NGROUPS = ([0, 1], [2, 3], [4])

        L1 = AR.mark()
        LVT = AR.alloc([7, NCT], F32); bLVT = Buf()
        HBI = AR.alloc([2, 2, NCT], F32); bHBI = Buf()
        C8 = AR.alloc([2, 2, NCT], F32); bC8 = Buf()
        HC = AR.alloc([2, NCT], F32); bHC = Buf()
        JK = AR.alloc([D], BF16); bJK = Buf()
        RAs = [AR.alloc([4, BLK], F32) for _ in range(2)]; bRA = [Buf(), Buf()]
        QQs = [AR.alloc([4, BLK], F32) for _ in range(2)]; bQQ = [Buf(), Buf()]
        TIs = [AR.alloc([4, BLK], F32) for _ in range(2)]; bTI = [Buf(), Buf()]
        ZC16s = [AR.alloc([4, BLK], BF16) for _ in range(2)]; bZC16 = [Buf(), Buf()]
        HFT = [AR.alloc([BLK], F32) for _ in range(2)]; bHFT = [Buf(), Buf()]
        WGA = [AR.alloc([2, 5, 256], BF16) for _ in range(2)]; bWGAs = [[Buf() for _ in range(5)] for _ in range(2)]
        S.dma("sp", LVT, lvT_d, writes=[bLVT])
        for d in range(2):
            for g in range(2):
                TS("dve", HBI[:, d, g, :], LVT[:, 1 + 3 * d + g, :], 0.5, ALU.mult, [bLVT], [bHBI])
            ACT(C8[:, d, 1, :], LVT[:, 3 + 3 * d, :], AF.Exp, [bLVT], [bC8], scale=-1.0)
            ACT(C8[:, d, 1, :], C8[:, d, 1, :], AF.Ln, [bC8], [bC8], bias=1.0)
            TS("dve", C8[:, d, 0, :], C8[:, d, 1, :], -4.0, ALU.mult, [bC8], [bC8])
            TS("dve", C8[:, d, 1, :], C8[:, d, 1, :], -8.0, ALU.mult, [bC8], [bC8])
        MEMSET("dve", HC, 0.0, [bHC])

        def load_gates(d):
            for g in range(2):
                for n in range(5):
                    S.dma("pool", WGA[g][:, :, n, :], lgate_d[2 * d + g, n].rearrange("(dt p) e -> p dt e", p=128),
                          writes=[bWGAs[g][n]])

        gcount = [0]

        def gate_parts(ns, N, d, z32, z16, rev, consume, slot):
            cos = [2 * n + et for n in ns for et in range(2)]
            RA = RAs[slot]; QQ = QQs[slot]; TI = TIs[slot]
            p1 = []
            p2 = []

            def mk1(gi, co):
                def f():
                    n = co // 2
                    et = co % 2
                    for g in range(2):
                        pb = 4 + g
                        for dt in range(2):
                            a16, b16 = z16(2 * n + dt)
                            MM(bank(pb)[:, 0:N], WGA[g][:, dt, n, et * 128:(et + 1) * 128], a16, dt == 0, dt == 1,
                               [bWGAs[g][n], b16], [psb[pb]])
                        dst, bdst = (RA, bRA[slot]) if g == 0 else (TI, bTI[slot])
                        ACT(dst[:, gi, 0:N], bank(pb)[:, 0:N], AF.Tanh, [psb[pb], bHBI], [bdst],
                            bias=HBI[:, d, g, co:co + 1], scale=0.5)
                    ACT(QQ[:, gi, 0:N], RA[:, gi, 0:N], AF.Exp, [bRA[slot], bC8], [bQQ[slot]],
                        bias=C8[:, d, 1, co:co + 1], scale=C8[:, d, 1, co:co + 1])
                    ACT(RA[:, gi, 0:N], RA[:, gi, 0:N], AF.Exp, [bRA[slot], bC8], [bRA[slot]],
                        bias=C8[:, d, 0, co:co + 1], scale=C8[:, d, 0, co:co + 1])
                return f

            def mksq():
                def f():
                    ng = len(cos)
                    ACT(QQ[:, 0:ng, 0:N], QQ[:, 0:ng, 0:N], AF.Sqrt, [bQQ[slot]], [bQQ[slot]], bias=1.0, scale=-1.0)
                return f

            def mk2(gi, co):
                def f():
                    a32, b32 = z32(co)
                    STT(TI[:, gi, 0:N], TI[:, gi, 0:N], 1.0, a32, ALU.add, ALU.mult, [bTI[slot], b32], [bTI[slot]])
                    STT(TI[:, gi, 0:N], QQ[:, gi, 0:N], 0.5, TI[:, gi, 0:N], ALU.mult, ALU.mult,
                        [bTI[slot], bQQ[slot]], [bTI[slot]])
                    hb = HFT[co % 2]; bhb = bHFT[co % 2]
                    if not rev:
                        SCAN(hb[:, 0:N], RA[:, gi, 0:N], TI[:, gi, 0:N], HC[:, d, co:co + 1],
                             [bRA[slot], bTI[slot], bHC], [bhb])
                        CP("dve", HC[:, d, co:co + 1], hb[:, N - 1:N], [bhb], [bHC])
                    else:
                        SCAN(hb[:, 0:N][:, ::-1], RA[:, gi, 0:N][:, ::-1], TI[:, gi, 0:N][:, ::-1], HC[:, d, co:co + 1],
                             [bRA[slot], bTI[slot], bHC], [bhb])
                        CP("dve", HC[:, d, co:co + 1], hb[:, 0:1], [bhb], [bHC])
                    if consume is not None:
                        consume[0](co, hb, bhb)
                return f

            def mk2b(gi, co):
                def f():
                    consume[1](co)
                return f
            for gi, co in enumerate(cos):
                p1.append(mk1(gi, co))
            p1.append(mksq())
            two = consume is not None and consume[1] is not None
            for gi, co in enumerate(cos):
                p2.append(mk2(gi, co))
                if two and gi >= 1:
                    p2.append(mk2b(gi - 1, cos[gi - 1]))
            if two:
                p2.append(mk2b(len(cos) - 1, cos[-1]))
            return p1, p2

        def merge(*lists):
            lists = [l for l in lists if l]
            if not lists:
                return []
            tot = max(len(l) for l in lists)
            items = []
            for li_, l in enumerate(lists):
                for i, t in enumerate(l):
                    items.append(((i + 0.5) / len(l), li_, i, t))
            items.sort(key=lambda x: (x[0], x[1], x[2]))
            return [t for _, _, _, t in items]

        def run(thunks):
            for t in thunks:
                t()

        L1P = AR.mark()

        WIN = AR.alloc([8, 2 * LW], BF16); bWINz = [Buf() for _ in range(8)]; bWINg = [Buf() for _ in range(8)]
        bWINs = bWINz + bWINg
        DIAG = AR.alloc([NCT, 4, 128], BF16); bDIAG = Buf()
        CW = AR.alloc([NCT, 4], F32); bCW = Buf()
        XT = AR.alloc([4, D], F32); bXT = Buf()
        HT = AR.alloc([8, BLK], BF16); bHT = Buf()
        ZH = [AR.alloc([NCT, 516], BF16) for _ in range(3)]
        bZH = [[Buf() for _ in range(NCT)] for _ in range(3)]
        TG = [AR.alloc([BLK], F32) for _ in range(2)]; bTG = [Buf(), Buf()]
        SGT1 = AR.alloc([4, BLK], BF16); bSGT1 = Buf()
        SGT = [SGT1, SGT1]; bSGT = [bSGT1, bSGT1]
        ZC32s = [AR.alloc([4, BLK], F32) for _ in range(2)]; bZC32 = [Buf(), Buf()]
        for dc in range(8):
            S.dma("pool", WIN[:, dc, 0:LW], lwin_d[dc * 128:(dc + 1) * 128, 0:LW], writes=[bWINz[dc]])
        bWOUTs = [Buf() for _ in range(NCT)]
        WOUT_E = WIN.rearrange("p a b -> p (a b)")[:, 0:NCT * D].rearrange("p (a b) -> p a b", a=NCT)
        load_gates(0)
        for dc in range(8):
            S.dma("pool", WIN[:, dc, LW:2 * LW], lwin_d[dc * 128:(dc + 1) * 128, LW:2 * LW], writes=[bWINg[dc]])
        S.dma("sp", CW, lcwT_d, writes=[bCW])
        for ct in range(NCT):
            for j in range(4):
                TS("dve", DIAG[:, ct, j, :], ID32, CW[:, ct, j:j + 1], ALU.mult, [bID32, bCW], [bDIAG])

        def stage1_z(nt, v, zslot, zprev, first, next_load):
            N = nt * 128
            zh = ZH[zslot]
            th = []

            def f_front():
                front_end(XT, bXT, HT, bHT, nt, 1, v, JK, bJK, (0, 1))
                if next_load is not None:
                    next_load()
            th.append(f_front)
            for ct in range(NCT):
                def f_z(ct=ct):
                    pb = ct % 2
                    for dc in range(8):
                        MM(bank(pb)[:, 0:N], WIN[:, dc, ct * 128:(ct + 1) * 128], HT[:, dc, 0:N], dc == 0, dc == 7,
                           [bWINz[dc], bHT], [psb[pb]])
                    CP("act", zh[:, ct, 2:2 + N], bank(pb)[:, 0:N], [psb[pb]], [bZH[zslot][ct]])
                th.append(f_z)

            def f_halo():
                if first:
                    MEMSET("pool", zh[:, :, 0:2], 0.0, bZH[zslot])
                else:
                    zp = ZH[zprev]
                    CP("pool", zh[:, :, 0:2], zp[:, :, 512:514], bZH[zprev], bZH[zslot])
                    CP("pool", zp[:, :, 514:515], zh[:, :, 2:3], bZH[zslot], bZH[zprev])
            th.append(f_halo)
            return th

        def stage1_g(k):
            th = []
            for gidx, ns in enumerate(NGROUPS):
                for gi0, n in enumerate(ns):
                    for dt in range(2):
                        def f(gidx=gidx, ns=ns, gi0=gi0, n=n, dt=dt):
                            ct = 2 * n + dt
                            gi = 2 * gi0 + dt
                            sg = SGT[gidx % 2]; bsg = bSGT[gidx % 2]
                            pb = 2 + dt
                            for dc in range(8):
                                MM(bank(pb), WIN[:, dc, LW + ct * 128:LW + (ct + 1) * 128], HT[:, dc, :], dc == 0, dc == 7,
                                   [bWINg[dc], bHT], [psb[pb]])
                            ACT(TG[dt], bank(pb), AF.Tanh, [psb[pb]], [bTG[dt]], scale=0.5)
                            STT(sg[:, gi, :], TG[dt], 1.0, bank(pb), ALU.add, ALU.mult, [bTG[dt], psb[pb]], [bsg])
                            if gi == 2 * len(ns) - 1:
                                S.dma("sp", sgl_s[k, gidx].rearrange("p (a b) -> p a b", a=4)[:, 0:2 * len(ns), :],
                                      sg[:, 0:2 * len(ns), :], reads=[bsg], writes=[bzs[2][k][gidx]])
                        th.append(f)
            return th

        def conv_thunks(zslot, N, ns, slot, store_blk, gidx, zcx):
            th = []
            for gi0, n in enumerate(ns):
                for dt in range(2):
                    def f(gi0=gi0, n=n, dt=dt):
                        ct = 2 * n + dt
                        gi = 2 * gi0 + dt
                        pb = 6 + dt
                        zh = ZH[zslot]
                        for j in range(4):
                            MM(bank(pb)[:, 0:N], DIAG[:, ct, j, :], zh[:, ct, j:j + N], j == 0, j == 3,
                               [bDIAG, bZH[zslot][ct]], [psb[pb]])
                        TS("dve", ZC32s[slot][:, gi, 0:N], bank(pb)[:, 0:N], LVT[:, 0, ct:ct + 1], ALU.add,
                           [psb[pb], bLVT], [bZC32[slot]])
                        TS("dve", ZC16s[slot][:, gi, 0:N], bank(pb)[:, 0:N], LVT[:, 0, ct:ct + 1], ALU.add,
                           [psb[pb], bLVT], [bZC16[slot]])
                        last = (gi == 2 * len(ns) - 1)
                        if last and store_blk is not None:
                            S.dma("sp", zc_s[store_blk, gidx].rearrange("p (a b) -> p a b", a=4)[:, 0:2 * len(ns), :],
                                  ZC32s[slot][:, 0:2 * len(ns), :], reads=[bZC32[slot]], writes=[bzs[0][store_blk][gidx]])
                        if zcx:
                            S.dma("sp", zcx_s[:, ct * CTX:(ct + 1) * CTX], ZC32s[slot][:, gi, 0:N], reads=[bZC32[slot]],
                                  writes=[bzcx_s])
                    th.append(f)
            return th

        pend = [[]]

        def do_group(zslot, N, gidx, store_blk, zcx, extra):
            ns = NGROUPS[gidx]
            slot = gcount[0] % 2
            gcount[0] += 1
            n0 = ns[0]
            z32 = lambda ct, slot=slot, n0=n0: (ZC32s[slot][:, ct - 2 * n0, 0:N], bZC32[slot])
            z16 = lambda ct, slot=slot, n0=n0: (ZC16s[slot][:, ct - 2 * n0, 0:N], bZC16[slot])
            if store_blk is not None:
                def consume0(co, hb, bhb):
                    S.dma("sp", hf_s[store_blk, co], hb, reads=[bhb], writes=[bzs[1][store_blk][co]])
                consume = (consume0, None)
            else:
                consume = None
            cv = conv_thunks(zslot, N, ns, slot, store_blk, gidx, zcx)
            p1, p2 = gate_parts(ns, N, 0, z32, z16, False, consume, slot)
            run(merge(cv + p1, pend[0], extra))
            pend[0] = p2

        S.dma("sp", XT[:, 0:2, :], ctx1_d.rearrange("(tt p) d -> p tt d", p=128), reads=[bx1[0]], writes=[bXT])
        MEMSET("pool", ZH[2], 0.0, bZH[2])

        def ld(k):
            def f():
                S.dma("sp", XT, x1c[:, 4 * k:4 * k + 4, :], writes=[bXT])
            return f
        def A_list(k):
            return stage1_z(4, 0, k % 3, (k - 1) % 3, k == 0, ld(k + 1) if k + 1 < NB else None) + stage1_g(k)
        run(stage1_z(2, 1, 2, 0, True, ld(0)))
        a0 = A_list(0)
        per0 = (len(a0) + 2) // 3
        for gidx in range(3):
            do_group(2, CTX, gidx, None, True, a0[gidx * per0:(gidx + 1) * per0])
        run(pend[0]); pend[0] = []
        _cut(1)
        for k in range(NB):
            al = A_list(k + 1) if k + 1 < NB else []
            if k + 1 == NB - 1:
                def f_wout():
                    for ct in range(NCT):
                        S.dma("pool", WOUT_E[:, ct, :], lwout_d[ct * 128:(ct + 1) * 128, :],
                              writes=[bWOUTs[ct]] + (bWINs if ct == 0 else []))
                al = al + [f_wout]
            if k >= 1:
                per = (len(al) + 2) // 3
                for gidx in range(3):
                    do_group((k - 1) % 3, BLK, gidx, k - 1, False, al[gidx * per:(gidx + 1) * per])
            else:
                run(al)
        MEMSET("pool", ZH[(NB - 1) % 3][:, :, 514:515], 0.0, bZH[(NB - 1) % 3])
        for gidx in range(3):
            do_group((NB - 1) % 3, BLK, gidx, NB - 1, False, [])
        run(pend[0]); pend[0] = []
        S.barrier()
        AR.reset(L1P)
        _cut(3)

        WOUT = AR.alloc([NCT, D], BF16)
        load_gates(1)
        GP1 = AR.alloc([D], F32); bGP1 = Buf()
        S.dma("sp", GP1, gpl_s[1][0], reads=[bgpl_s[1][0]], writes=[bGP1])
        LX = AR.alloc([4, D], F32); bLX = Buf()
        LZC = [AR.alloc([4, BLK], F32) for _ in range(3)]; bLZC = [Buf(), Buf(), Buf()]
        LHF = [AR.alloc([4, BLK], F32) for _ in range(3)]; bLHF = [Buf(), Buf(), Buf()]
        LSG = [AR.alloc([4, BLK], BF16) for _ in range(3)]; bLSG = [Buf(), Buf(), Buf()]
        SUM = [AR.alloc([BLK], F32) for _ in range(2)]; bSUM = [Buf(), Buf()]
        HG0 = AR.alloc([NCT, BLK], BF16)
        TMPY = AR.alloc([D], F32); bTMPY = Buf()
        mk_ = AR.mark()
        ZCX = AR.alloc([NCT, CTX], F32); bZCX = Buf()
        AR.reset(mk_)
        HG1 = AR.alloc([NCT, BLK], BF16)
        HGs = [HG0, HG1]; bHGs = [Buf(), bZCX]
        S.dma("sp", ZCX.rearrange("p a b -> p (a b)"), zcx_s, reads=[bzcx_s], writes=[bZCX])

        for gidx, ns in enumerate(NGROUPS):
            n0 = ns[0]
            slot = gcount[0] % 2
            gcount[0] += 1
            for gi, ct in enumerate([2 * n + dt for n in ns for dt in range(2)]):
                CP("dve", ZC16s[slot][:, gi, 0:CTX], ZCX[:, ct, :], [bZCX], [bZC16[slot]])
            p1, p2 = gate_parts(ns, CTX, 1, lambda ct: (ZCX[:, ct, :], bZCX),
                                lambda ct, n0=n0, slot=slot: (ZC16s[slot][:, ct - 2 * n0, 0:CTX], bZC16[slot]), True, None, slot)
            run(p1); run(p2)

        seq = [(k, gidx) for k in range(NB - 1, -1, -1) for gidx in range(3)]
        _cut(4)

        def loadG(si):
            k, gidx = seq[si]
            slot = si % 3
            ng = 2 * len(NGROUPS[gidx])
            S.dma("sp", LZC[slot][:, 0:ng, :], zc_s[k, gidx].rearrange("p (a b) -> p a b", a=4)[:, 0:ng, :],
                  reads=[bzs[0][k][gidx]], writes=[bLZC[slot]])
            S.dma("sp", LSG[slot][:, 0:ng, :], sgl_s[k, gidx].rearrange("p (a b) -> p a b", a=4)[:, 0:ng, :],
                  reads=[bzs[2][k][gidx]], writes=[bLSG[slot]])
            for gi in range(ng):
                co = 2 * NGROUPS[gidx][0] + gi
                S.dma("sp", LHF[slot][:, gi, :], hf_s[k, co], reads=[bzs[1][k][co]], writes=[bLHF[slot]])

        def outproj_thunks(k):
            th = []
            HG = HGs[k % 2]; bHG = bHGs[k % 2]
            for tt in range(4):
                def f(tt=tt):
                    b0 = 0 if tt % 2 == 0 else 2
                    for hf in range(2):
                        for ct in range(NCT):
                            MM(bank(b0 + hf), HG[:, ct, tt * 128:(tt + 1) * 128], WOUT[:, ct, hf * 512:(hf + 1) * 512],
                               ct == 0, ct == NCT - 1, [bHG, bWOUTs[ct]], [psb[b0 + hf]])
                    back_end(PS[:, b0 * 512:b0 * 512 + 1024], [psb[b0], psb[b0 + 1]], LX[:, tt, :], bLX, GP1, bGP1,
                             TMPY, bTMPY, JK, bJK, li=1)
                    if tt == 3:
                        S.dma("sp", outc[:, 4 * k:4 * k + 4, :], LX, reads=[bLX], writes=[Buf()])
                        if k > 0:
                            S.dma("sp", LX, x1c[:, 4 * (k - 1):4 * (k - 1) + 4, :], writes=[bLX])
                th.append(f)
            return th

        loadG(0)
        S.dma("sp", LX, x1c[:, 4 * (NB - 1):4 * (NB - 1) + 4, :], writes=[bLX])
        pend2 = []
        opq = []
        for si, (k, gidx) in enumerate(seq):
            slot = si % 3
            ns = NGROUPS[gidx]
            n0 = ns[0]
            gslot = gcount[0] % 2
            gcount[0] += 1
            if si == 4:
                _cut(5)
            if si + 1 < len(seq):
                loadG(si + 1)
            cast = []
            for gi in range(2 * len(ns)):
                def fc(gi=gi, slot=slot, gslot=gslot):
                    CP("act", ZC16s[gslot][:, gi, :], LZC[slot][:, gi, :], [bLZC[slot]], [bZC16[gslot]])
                cast.append(fc)

            def consume_a(co, hb, bhb, slot=slot, n0=n0):
                gi = co - 2 * n0
                sm = SUM[co % 2]; bsm = bSUM[co % 2]
                TT("dve", sm, hb, LHF[slot][:, gi, :], ALU.add, [bhb, bLHF[slot]], [bsm])

            def consume_b(co, slot=slot, n0=n0, k=k):
                gi = co - 2 * n0
                sm = SUM[co % 2]; bsm = bSUM[co % 2]
                STT(HGs[k % 2][:, co, :], sm, 0.5, LSG[slot][:, gi, :], ALU.mult, ALU.mult, [bsm, bLSG[slot]], [bHGs[k % 2]])
            consume = (consume_a, consume_b)
            p1, p2 = gate_parts(ns, BLK, 1, lambda ct, slot=slot, n0=n0: (LZC[slot][:, ct - 2 * n0, :], bLZC[slot]),
                                lambda ct, n0=n0, gslot=gslot: (ZC16s[gslot][:, ct - 2 * n0, :], bZC16[gslot]), True, consume, gslot)
            ex_now = []
            if opq and gidx >= 1:
                n_take = 2 if gidx == 1 else len(opq)
                ex_now = opq[:n_take]
                opq = opq[n_take:]
            run(merge(cast + p1, pend2, ex_now))
            pend2 = p2
            if gidx == 0 and si > 0:
                opq = outproj_thunks(k + 1)
        run(pend2)
        run(outproj_thunks(0))
        AR.reset(L1)

    if do1:
        try:
            _lru()
        except _Cut:
            pass

    S.barrier()
    if needed is not None:
        S.emit()
    return nc, S.used


def _colT(vec, n):
    return np.ascontiguousarray(np.asarray(vec, np.float32).reshape(n, 128).T)


def _consts():
    ident = np.eye(128, dtype=np.float32)
    cm = np.ones((128, 2, BLK), np.float32)
    cm[:, 0, 0::128] = 0.0
    cm[:, 1, 127::128] = 0.0
    j = np.arange(128)[:, None]
    i = np.arange(128)[None, :]
    tm = np.zeros((128, 2, 4, 128), np.float32)
    tm[:, 0, :, :] = (j <= i).astype(np.float32)[:, None, :]
    tm[:, 1, :, :] = (j >= i).astype(np.float32)[:, None, :]
    return ident, cm, tm


def _common_inputs(b, inp):
    ident, cm, tm = _consts()
    cvec = np.stack([_colT(inp["c"][b], 8), _colT(inp["c_ctx"], 8)], axis=-1)
    ada_bT = np.stack([_colT(inp["ada_b"][i], 24) for i in range(2)], axis=1)
    npreT = np.stack([_colT(inp["norm_pre"][i], 8) for i in range(2)], axis=1)
    return {
        "cvec": np.ascontiguousarray(cvec), "ada_w": inp["ada_w"], "ada_bT": np.ascontiguousarray(ada_bT),
        "ada_b": inp["ada_b"], "npreT": np.ascontiguousarray(npreT), "norm_post": inp["norm_post"],
        "ident": ident, "cmask": cm, "tmask": tm,
    }


def _gla_inputs(inp):
    bgT = np.stack([_colT(inp["gla_bg_f"][0], 4), _colT(inp["gla_bg_b"][0], 4)], axis=1)
    return {
        "gla_w_in": inp["gla_w_in"][0],
        "gla_wg": np.ascontiguousarray(np.stack([inp["gla_wg_f"][0], inp["gla_wg_b"][0]], axis=0)),
        "gla_bgT": np.ascontiguousarray(bgT),
        "gla_normT": _colT(inp["gla_norm"][0], 2),
        "gla_w_out": inp["gla_w_out"][0],
    }


def _lru_inputs(inp):
    vT = np.stack([_colT(inp[k][0], NCT) for k in
                   ("lru_conv_b", "lru_ba_f", "lru_bx_f", "lru_lam_f", "lru_ba_b", "lru_bx_b", "lru_lam_b")], axis=1)
    cwT = np.stack([_colT(inp["lru_conv_w"][0, j], NCT) for j in range(4)], axis=-1)
    gates = np.stack([inp["lru_wa_f"][0], inp["lru_wx_f"][0], inp["lru_wa_b"][0], inp["lru_wx_b"][0]], axis=0)
    return {
        "lru_w_in": inp["lru_w_in"][0], "lru_cwT": np.ascontiguousarray(cwT), "lru_vT": np.ascontiguousarray(vT),
        "lru_gates": np.ascontiguousarray(gates), "lru_w_out": inp["lru_w_out"][0],
    }


_NC_CACHE = {}


def _get_nc(layers):
    key = tuple(layers)
    if key not in _NC_CACHE:
        _NC_CACHE[key] = build(layers)
    return _NC_CACHE[key]


def kernel(**inputs):
    inp = {k: np.asarray(v, np.float32) for k, v in inputs.items()}
    ncores = 8
    maps = []
    for core in range(ncores):
        b = core // 2
        m = {"x": np.ascontiguousarray(inp["x"][b]), "ctx": np.ascontiguousarray(inp["ctx"][b])}
        m.update(_common_inputs(b, inp))
        m.update(_gla_inputs(inp))
        m.update(_lru_inputs(inp))
        maps.append(m)
    nc = _get_nc((0, 1))
    res = run_bass_kernel_spmd(nc, maps, core_ids=list(range(ncores)))
    out = np.stack([res.results[2 * b]["out"] for b in range(4)], axis=0)
    return out.astype(np.float32)
```

```python
import math
import os
import numpy as np
import ml_dtypes
import concourse.bass as bass
import concourse.mybir as mybir
from concourse.bass_utils import run_bass_kernel_spmd

F32 = mybir.dt.float32
BF16 = mybir.dt.bfloat16
AF = mybir.ActivationFunctionType
ALU = mybir.AluOpType
AX = mybir.AxisListType

D = 1024
T = 8192
CTX = 256
NB = 16
BLK = 512
EPS = 1e-6
GIN = 3104
LW = 1280
NCT = 10


class Buf:
    __slots__ = ("name", "w", "r")

    def __init__(self, name=""):
        self.name = name
        self.w = None
        self.r = []


class Sched:
    ENGS = ("pe", "dve", "act", "pool", "sp")
    RING = 8

    def __init__(self, nc, needed=None):
        self.nc = nc
        self.needed = needed
        self.used = set()
        self.rank = {}
        self.nsig = {e: 0 for e in self.ENGS}
        self.streams = {e: [] for e in self.ENGS}
        self.count = {e: 0 for e in self.ENGS}
        self.sems = {}
        for e in ("pe", "dve", "act", "pool"):
            self.sems[e] = nc.alloc_semaphore("s_" + e)
        self.dma_n = {}
        for q in ("sp", "act", "pool"):
            self.dma_n[q] = 0
            for k in range(self.RING):
                self.sems[("dma", q, k)] = nc.alloc_semaphore("d_%s%d" % (q, k))
        self.seen = {e: {} for e in self.ENGS}
        self.n_inst = {e: 0 for e in self.ENGS}

    def _need(self, eng, reads, writes):
        need = {}

        def add(tok):
            if tok is None:
                return
            k, v = tok
            if k == "pe" and eng == "pe":
                return
            if need.get(k, 0) < v:
                need[k] = v
        for b in reads:
            add(b.w)
        for b in writes:
            add(b.w)
            for t in b.r:
                add(t)
        out = []
        seen = self.seen[eng]
        for k, v in need.items():
            if seen.get(k, 0) < v:
                seen[k] = v
                out.append((self.sems[k], self._phys(k, v)))
        return out

    def _phys(self, k, v):
        if isinstance(k, tuple):
            return v
        self.used.add((k, v))
        if self.needed is None:
            return v
        return self.rank[(k, v)]

    def _mark(self, tok, reads, writes):
        for b in reads:
            b.r.append(tok)
            if len(b.r) > 24:
                m = {}
                for k, v in b.r:
                    if m.get(k, 0) < v:
                        m[k] = v
                b.r = list(m.items())
        for b in writes:
            b.w = tok
            b.r = []

    def op(self, eng, fn, reads=(), writes=(), signal=True):
        waits = self._need(eng, reads, writes)
        if signal:
            self.count[eng] += 1
            tok = (eng, self.count[eng])
            if self.needed is not None:
                if tok in self.needed:
                    self.nsig[eng] += 1
                    self.rank[tok] = self.nsig[eng]
                else:
                    signal = False
        else:
            tok = (eng, self.count[eng] + 1)
        sem = self.sems[eng]
        self.n_inst[eng] += 1

        def run(e, fn=fn, waits=waits, signal=signal, sem=sem):
            for s, v in waits:
                e.wait_ge(s, v)
            ins = fn(e)
            if signal:
                ins.then_inc(sem, 1)
        self.streams[eng].append(run)
        self._mark(tok, reads, writes)

    def dma(self, q, out_ap, in_ap, reads=(), writes=()):
        i = self.dma_n[q]
        self.dma_n[q] += 1
        k = ("dma", q, i % self.RING)
        base = 16 * (i // self.RING)
        waits = self._need(q, reads, writes)
        seen = self.seen[q]
        if base > 0 and seen.get(k, 0) < base:
            seen[k] = base
            waits.append((self.sems[k], base))
        tok = (k, base + 16)
        sem = self.sems[k]
        self.n_inst[q] += 1

        def run(e, waits=waits, sem=sem, out_ap=out_ap, in_ap=in_ap):
            for s, v in waits:
                e.wait_ge(s, v)
            e.dma_start(out=out_ap, in_=in_ap).then_inc(sem, 16)
        self.streams[q].append(run)
        self._mark(tok, reads, writes)

    def _all_tokens(self):
        toks = []
        for e in ("pe", "dve", "act", "pool"):
            if self.count[e] > 0:
                toks.append((e, self.count[e]))
        for q in ("sp", "act", "pool"):
            n = self.dma_n[q]
            for k in range(self.RING):
                nk = (n - k + self.RING - 1) // self.RING if n > k else 0
                if nk > 0:
                    toks.append((("dma", q, k), 16 * nk))
        return toks

    def barrier(self):
        toks = self._all_tokens()
        for eng in self.ENGS:
            waits = []
            seen = self.seen[eng]
            for k, v in toks:
                if seen.get(k, 0) < v:
                    seen[k] = v
                    waits.append((self.sems[k], self._phys(k, v)))

            def run(e, waits=waits):
                for s, v in waits:
                    e.wait_ge(s, v)
            self.streams[eng].append(run)

    def emit(self):
        nc = self.nc
        with nc.Block() as block:
            @block.tensor
            def _(e):
                for f in self.streams["pe"]:
                    f(e)

            @block.vector
            def _(e):
                for f in self.streams["dve"]:
                    f(e)

            @block.scalar
            def _(e):
                for f in self.streams["act"]:
                    f(e)

            @block.gpsimd
            def _(e):
                for f in self.streams["pool"]:
                    f(e)

            @block.sync
            def _(e):
                for f in self.streams["sp"]:
                    f(e)


class _Cut(Exception):
    pass


def _cut(n):
    if int(os.environ.get('LRU_CUT', '0')) == n:
        raise _Cut()


class Arena:
    def __init__(self, nc, name, nbytes):
        self.t = nc.alloc_sbuf_tensor(name, [128, nbytes // 4], F32)
        self.cap = nbytes
        self.off = 0

    def alloc(self, free_shape, dtype):
        n = 1
        for s in free_shape:
            n *= s
        sz = n * (2 if dtype == BF16 else 4)
        sz = (sz + 31) // 32 * 32
        assert self.off + sz <= self.cap, ("arena overflow", self.off, sz, self.cap)
        v = self.t[:, self.off // 4:(self.off + sz) // 4]
        if dtype == BF16:
            v = v.bitcast(BF16)
        v = v[:, 0:n]
        if len(free_shape) == 2:
            v = v.rearrange("p (a b) -> p a b", a=free_shape[0])
        elif len(free_shape) == 3:
            v = v.rearrange("p (a b c) -> p a b c", a=free_shape[0], b=free_shape[1])
        self.off += sz
        return v

    def mark(self):
        return self.off

    def reset(self, m):
        self.off = m


def build(layers=(0, 1)):
    _, used = _build(layers, None)
    nc, _ = _build(layers, used)
    return nc


def _build(layers, needed):
    nc = bass.Bass("TRN2", target_bir_lowering=False)
    S = Sched(nc, needed)
    do0 = 0 in layers
    do1 = 1 in layers

    def din(name, shape, dt=F32):
        return nc.dram_tensor(name, list(shape), dt, kind="ExternalInput").ap()

    def dscr(name, shape, dt=F32):
        return nc.dram_tensor(name, list(shape), dt, kind="Internal").ap()

    def dout(name, shape, dt=F32):
        return nc.dram_tensor(name, list(shape), dt, kind="ExternalOutput").ap()

    x_d = din("x", [T, D])
    ctx_d = din("ctx", [CTX, D])
    cvec_d = din("cvec", [128, 8, 2])
    ada_w_d = din("ada_w", [2, D, 3 * D])
    ada_bT_d = din("ada_bT", [128, 2, 24])
    ada_b_d = din("ada_b", [2, 3 * D])
    npreT_d = din("npreT", [128, 2, 8])
    npost_d = din("norm_post", [2, D])
    ident_d = din("ident", [128, 128])
    cmask_d = din("cmask", [128, 2, BLK])
    tmask_d = din("tmask", [128, 2, 4, 128])
    if do0:
        gwin_d = din("gla_w_in", [D, GIN])
        gwg_d = din("gla_wg", [2, 16, 512])
        gbgT_d = din("gla_bgT", [128, 2, 4])
        gnT_d = din("gla_normT", [128, 2])
        gwout_d = din("gla_w_out", [D, D])
    if do1:
        lwin_d = din("lru_w_in", [D, 2 * LW])
        lcwT_d = din("lru_cwT", [128, NCT, 4])
        lvT_d = din("lru_vT", [128, 7, NCT])
        lgate_d = din("lru_gates", [4, 5, 256, 256])
        lwout_d = din("lru_w_out", [LW, D])
    if do0 and do1:
        x1_d = dscr("x1", [T, D])
        ctx1_d = dscr("ctx1", [CTX, D])
    elif do0:
        x1_d = dout("x1", [T, D])
        ctx1_d = dout("ctx1", [CTX, D])
    else:
        x1_d = x_d
        ctx1_d = ctx_d
    if do1:
        out_d = dout("out", [T, D])

    bx1 = [Buf("x1_%d" % i) for i in range(NB + 1)]

    AR = Arena(nc, "arena", 211968)
    PS = nc.alloc_psum_tensor("psum", [128, 4096], F32)
    psb = [Buf("ps%d" % i) for i in range(8)]

    def bank(i, n=512):
        return PS[:, i * 512:i * 512 + n]

    def MM(out, lhsT, rhs, start, stop, r, w, sig=None):
        S.op("pe", lambda e: e.matmul(out, lhsT, rhs, start=start, stop=stop), r, w,
             signal=(stop if sig is None else sig))

    def TR(out, in_, ident, r, w, sig):
        S.op("pe", lambda e: e.transpose(out, in_, ident), r, w, signal=sig)

    def ACT(out, in_, func, r, w, bias=None, scale=None, accum=None):
        kw = {}
        if bias is not None:
            kw["bias"] = bias
        if scale is not None:
            kw["scale"] = scale
        if accum is not None:
            kw["accum_out"] = accum
        S.op("act", lambda e: e.activation(out, in_, func, **kw), r, w)

    def TT(eng, out, a, b, op, r, w):
        S.op(eng, lambda e: e.tensor_tensor(out, a, b, op), r, w)

    def TS(eng, out, a, s1, op0, r, w, s2=None, op1=None):
        if op1 is None:
            S.op(eng, lambda e: e.tensor_scalar(out, a, s1, None, op0), r, w)
        else:
            S.op(eng, lambda e: e.tensor_scalar(out, a, s1, s2, op0, op1), r, w)

    def STT(out, in0, scalar, in1, op0, op1, r, w):
        S.op("dve", lambda e: e.scalar_tensor_tensor(out, in0, scalar, in1, op0, op1), r, w)

    def CP(eng, out, in_, r, w):
        if eng == "act":
            S.op("act", lambda e: e.copy(out, in_), r, w)
        else:
            S.op(eng, lambda e: e.tensor_copy(out, in_), r, w)

    def SCAN(out, d0, d1, init, r, w):
        S.op("dve", lambda e: e.tensor_tensor_scan(out, d0, d1, init, ALU.mult, ALU.add), r, w)

    def MEMSET(eng, ap, val, w):
        S.op(eng, lambda e: e.memset(ap, val), (), w)

    def RECIP(out, in_, r, w):
        S.op("dve", lambda e: e.reciprocal(out, in_), r, w)

    ID32 = AR.alloc([128], F32); bID32 = Buf()
    ID16 = AR.alloc([128], BF16); bID16 = Buf()
    ONES16 = AR.alloc([128], BF16); bONES = Buf()
    TM = AR.alloc([2, 4, 128], BF16); bTM = Buf()
    CV = AR.alloc([8, 2], F32); bCV = Buf()
    MODT = AR.alloc([2, 24, 2], F32); bMODT = Buf()
    ABT = AR.alloc([2, 24], F32); bABT = Buf()
    NPT = AR.alloc([2, 8], F32); bNPT = Buf()
    GV = AR.alloc([2, 2, 8], F32); bGV = Buf()
    gpl_s = [[dscr("gpl_s%d%d" % (i, v), [128, D]) for v in range(2)] for i in range(2)]
    bgpl_s = [[Buf() for v in range(2)] for i in range(2)]
    SS = AR.alloc([8], F32); bSS = Buf()
    RS = AR.alloc([8], F32); bRS = Buf()
    SSY = AR.alloc([4], F32); bSSY = Buf()
    MHALF = AR.alloc([8], F32); bMH = Buf()

    S.dma("sp", ID32, ident_d, writes=[bID32])
    S.dma("sp", CV, cvec_d, writes=[bCV])
    S.dma("sp", ABT, ada_bT_d, writes=[bABT])
    S.dma("sp", NPT, npreT_d, writes=[bNPT])
    CP("dve", ID16, ID32, [bID32], [bID16])
    MEMSET("dve", ONES16, 1.0, [bONES])
    MEMSET("dve", MHALF, -0.5, [bMH])
    ACT(CV, CV, AF.Silu, [bCV], [bCV])

    def run_prologue():
        mk0 = AR.mark()
        TM32 = AR.alloc([2, 4, 128], F32); bTM32 = Buf()
        S.dma("sp", TM32, tmask_d, writes=[bTM32])
        CP("dve", TM, TM32, [bTM32], [bTM])
        SEL = AR.alloc([2, 128], F32); bSEL = Buf()
        for v in range(2):
            CP("dve", SEL[0:2, v, :], ID32[0:2, v:v + 1].broadcast_to([2, 128]), [bID32], [bSEL])
        AWs = [AR.alloc([8, 512], F32) for _ in range(2)]; bAWs = [Buf(), Buf()]
        GPL = [[AR.alloc([D], F32) for v in range(2)] for i in range(2)]
        bGPL = [[Buf() for v in range(2)] for i in range(2)]
        MR = AR.alloc([3 * D], F32); bMR = Buf()
        ABROW = AR.alloc([3 * D], F32); bABROW = Buf()
        NPR = AR.alloc([D], F32); bNPR = Buf()
        nch = 0
        for li in layers:
            S.dma("sp", ABROW[0:2, :], ada_b_d[li:li + 1, :].broadcast_to([2, 3 * D]), writes=[bABROW])
            S.dma("sp", NPR, npost_d[li:li + 1, :].broadcast_to([128, D]), writes=[bNPR])
            for ch in range(6):
                AW = AWs[nch % 2]; bAW = bAWs[nch % 2]
                nch += 1
                S.dma("sp", AW, ada_w_d[li, :, ch * 512:(ch + 1) * 512].rearrange("(dc p) n -> p dc n", p=128),
                      writes=[bAW])
                pb = ch % 2
                for dc in range(8):
                    MM(bank(pb)[0:2, :], CV[:, dc, :], AW[:, dc, :], dc == 0, dc == 7, [bAW, bCV], [psb[pb]])
                TT("dve", MR[0:2, ch * 512:(ch + 1) * 512], bank(pb)[0:2, :], ABROW[0:2, ch * 512:(ch + 1) * 512], ALU.add,
                   [psb[pb], bABROW], [bMR])
            for j in range(16):
                TR(bank(2)[:, 2 * j:2 * j + 2], MR[0:2, j * 128:(j + 1) * 128], ID32[0:2, 0:2], [bMR, bID32], [psb[2]], j == 15)
            CP("dve", MODT[:, li, 0:16, :], bank(2)[:, 0:32].rearrange("p (a b) -> p a b", b=2), [psb[2]], [bMODT])
            for v in range(2):
                for hf in range(2):
                    pg = bank(4 + hf)
                    MM(pg, SEL[0:2, v, :], MR[0:2, 2 * D + hf * 512:2 * D + (hf + 1) * 512], True, True, [bSEL, bMR], [psb[4 + hf]])
                    cs = slice(hf * 512, (hf + 1) * 512)
                    TT("dve", GPL[li][v][:, cs], pg, NPR[:, cs], ALU.mult, [psb[4 + hf], bNPR], [bGPL[li][v]])
                STT(GV[:, li, v, :], MODT[:, li, 8:16, v], 1.0, NPT[:, li, :], ALU.add, ALU.mult,
                    [bMODT, bNPT], [bGV])
                S.dma("sp", gpl_s[li][v], GPL[li][v], reads=[bGPL[li][v]], writes=[bgpl_s[li][v]])
        S.barrier()
        AR.reset(mk0)

    if not do0:
        run_prologue()

    def front_end(XT, bXT, HT, bHT, nt, li, v, JK, bJK, pa_banks):
        N = nt * 128
        for tt in range(nt):
            ACT(JK, XT[:, tt, :], AF.Square, [bXT], [bJK, bSS], accum=SS[:, tt:tt + 1])
        if li == 0:
            ACT(RS[:, 0:nt], SS[:, 0:nt], AF.Ln, [bSS], [bRS], bias=EPS, scale=1.0 / D)
            ACT(RS[:, 0:nt], RS[:, 0:nt], AF.Exp, [bRS], [bRS], scale=-0.5)
        else:
            TS("pool", RS[:, 0:nt], SS[:, 0:nt], 1.0 / D, ALU.mult, [bSS], [bRS], s2=EPS, op1=ALU.add)
            TT("pool", RS[:, 0:nt], RS[:, 0:nt], MHALF[:, 0:nt], ALU.pow, [bRS, bMH], [bRS])
        for tt in range(nt):
            TS("dve", XT[:, tt, :], XT[:, tt, :], RS[:, tt:tt + 1], ALU.mult, [bXT, bRS], [bXT])
        for dc in range(8):
            pb = pa_banks[dc % 2]
            for tt in range(nt):
                TR(bank(pb)[:, tt * 128:(tt + 1) * 128], XT[:, tt, dc * 128:(dc + 1) * 128], ID32,
                   [bXT, bID32], [psb[pb]], tt == nt - 1)
            ACT(HT[:, dc, 0:N], bank(pb)[:, 0:N], AF.Identity, [psb[pb], bGV, bMODT], [bHT],
                bias=MODT[:, li, dc, v:v + 1], scale=GV[:, li, v, dc:dc + 1])

    def back_end(py, pyb, XTt, bXT, GP, bGP, TMPY, bTMPY, JK, bJK, li=0):
        ACT(JK, py, AF.Square, pyb, [bJK, bSSY], accum=SSY[:, 0:1])
        if li == 0:
            ACT(SSY[:, 1:2], SSY[:, 0:1], AF.Ln, [bSSY], [bSSY], bias=EPS, scale=1.0 / D)
            ACT(SSY[:, 2:3], SSY[:, 1:2], AF.Exp, [bSSY], [bSSY], scale=-0.5)
        else:
            TS("pool", SSY[:, 1:2], SSY[:, 0:1], 1.0 / D, ALU.mult, [bSSY], [bSSY], s2=EPS, op1=ALU.add)
            TT("pool", SSY[:, 2:3], SSY[:, 1:2], MHALF[:, 0:1], ALU.pow, [bSSY, bMH], [bSSY])
        STT(TMPY, py, SSY[:, 2:3], GP, ALU.mult, ALU.mult, pyb + [bSSY, bGP], [bTMPY])
        TT("dve", XTt, TMPY, XTt, ALU.add, [bTMPY, bXT], [bXT])

    if do0:
        NBLK = NB + 1
        qtb_s = dscr("qtb_s", [NBLK, 128, 4 * BLK], BF16)
        ktb_s = dscr("ktb_s", [NBLK, 128, 4 * BLK], BF16)
        ktok_s = dscr("ktok_s", [NBLK, 128, 4 * 512], BF16)
        vv_s = dscr("vv_s", [NBLK, 128, 4 * D], BF16)
        sg_s = dscr("sg_s", [NBLK, 128, 8 * BLK], BF16)
        of_s = dscr("of_s", [NBLK, 128, 4 * 1024], F32)
        bsc = [[Buf() for _ in range(NBLK)] for _ in range(6)]

        L0 = AR.mark()
        WOUT = AR.alloc([8, D], BF16); bWOUT = Buf()
        CM = AR.alloc([2, BLK], F32); bCM = Buf()
        S.dma("sp", CM, cmask_d, writes=[bCM])
        WG = AR.alloc([2, 512], BF16); bWG = Buf()
        BGN = AR.alloc([2, 4], F32); bBGN = Buf()
        GN = AR.alloc([2], F32); bGN = Buf()
        ELB = AR.alloc([4, 4 * NBLK], F32); bELB = Buf()
        S32 = AR.alloc([4, 256], F32); bS32 = Buf()
        S16 = AR.alloc([4, 256], BF16); bS16 = Buf()
        T1 = AR.alloc([4, 256], F32); bT1 = Buf()
        SCM = AR.alloc([4, 128], BF16); bSCM = Buf()
        JK = AR.alloc([D], BF16); bJK = Buf()
        GPL = [[AR.alloc([D], F32) for v in range(2)], None]
        bGPL = [[Buf(), Buf()], None]
        for vc in range(8):
            S.dma("pool", WOUT[:, vc, :], gwout_d[vc * 128:(vc + 1) * 128, :], writes=[bWOUT])
        for d in range(2):
            S.dma("pool", WG[0:16, d, :], gwg_d[d], writes=[bWG])
        S.dma("sp", BGN, gbgT_d, writes=[bBGN])
        S.dma("sp", GN, gnT_d, writes=[bGN])
        TS("dve", BGN, BGN, -1.0, ALU.mult, [bBGN], [bBGN])
        SCM2 = AR.alloc([4, 128], BF16)
        L0P = AR.mark()
        WQ = AR.alloc([8, GIN], BF16); bWQ = Buf()
        for dc in range(8):
            S.dma("pool", WQ[:, dc, :], gwin_d[dc * 128:(dc + 1) * 128, :], writes=[bWQ])
        run_prologue()
        for v in range(2):
            S.dma("sp", GPL[0][v], gpl_s[0][v], reads=[bgpl_s[0][v]], writes=[bGPL[0][v]])

        def blk_src(bi):
            if bi == 0:
                return ctx_d.rearrange("(tt p) d -> p tt d", p=128), 2
            s = (bi - 1) * BLK
            return x_d[s:s + BLK, :].rearrange("(tt p) d -> p tt d", p=128), 4

        SCMs = [SCM, SCM2]; bSCMs = [bSCM, Buf()]

        def sc_pre(ci, par, QT, KT, KTOK, V, bins, mask_i):
            for hd in range(4):
                MM(bank(2)[:, hd * 128:(hd + 1) * 128], KT[:, hd, ci * 128:(ci + 1) * 128],
                   QT[:, hd, ci * 128:(ci + 1) * 128], True, True, bins, [psb[2]], sig=(hd == 3))
            TT("dve", SCMs[par], bank(2).rearrange("p (a b) -> p a b", a=4), TM[:, mask_i], ALU.mult,
               [psb[2], bTM], [bSCMs[par]])
            for hd in range(4):
                MM(PS[:, 5 * 512 + hd * 256: 5 * 512 + (hd + 1) * 256], KTOK[:, ci, hd * 128:(hd + 1) * 128],
                   V[:, ci, hd * 256:(hd + 1) * 256], True, True, bins, [psb[5 + hd // 2]], sig=(hd % 2 == 1))

        def sc_out(ci, par, QT, V, bins, pot_out):
            for hd in range(4):
                for vv in range(2):
                    o = PS[:, 3 * 512 + (hd * 2 + vv) * 128: 3 * 512 + (hd * 2 + vv + 1) * 128]
                    pb = psb[3 + (hd // 2)]
                    MM(o, S16[:, hd, vv * 128:(vv + 1) * 128], QT[:, hd, ci * 128:(ci + 1) * 128],
                       True, False, bins + [bS16], [pb])
                    MM(o, V[:, ci, hd * 256 + vv * 128: hd * 256 + (vv + 1) * 128], SCMs[par][:, hd, :],
                       False, True, bins + [bSCMs[par]], [pb], sig=(vv == 1 and hd % 2 == 1))
            pot_out()

        def sc_upd(EL, bEL, elcol):
            TT("dve", T1, PS[:, 5 * 512:7 * 512].rearrange("p (a b) -> p a b", a=4), S32, ALU.add,
               [psb[5], psb[6], bS32], [bT1])
            TT("dve", S32, T1, EL[:, :, elcol:elcol + 1].broadcast_to([128, 4, 256]), ALU.mult,
               [bT1, bEL], [bS32])
            CP("act", S16, S32, [bS32], [bS16])

        def scan_thunks(cis, QT, KT, KTOK, V, bins, mask_i, EL, bEL, elcol_of, pot_of):
            th = []
            n = len(cis)
            th.append(lambda: sc_pre(cis[0], 0, QT, KT, KTOK, V, bins, mask_i))
            for i, ci in enumerate(cis):
                th.append(lambda i=i, ci=ci: sc_out(ci, i % 2, QT, V, bins, pot_of(ci)))
                th.append(lambda i=i, ci=ci: sc_upd(EL, bEL, elcol_of(ci)))
                if i + 1 < n:
                    th.append(lambda i=i: sc_pre(cis[i + 1], (i + 1) % 2, QT, KT, KTOK, V, bins, mask_i))
            return th

        XT = AR.alloc([4, D], F32); bXT = Buf()
        HT = AR.alloc([8, BLK], BF16); bHT = Buf()
        LR16 = AR.alloc([2, BLK], BF16); bLR = Buf()
        EB = [AR.alloc([2, BLK], F32) for _ in range(2)]; bEB = [Buf(), Buf()]
        CB = [AR.alloc([2, BLK], F32) for _ in range(2)]; bCB = [Buf(), Buf()]
        QTb = AR.alloc([4, BLK], BF16); bQTb = Buf()
        KTb = AR.alloc([4, BLK], BF16); bKTb = Buf()
        KTOKb = AR.alloc([4, 512], BF16); bKTOKb = Buf()
        QTf = [AR.alloc([4, BLK], BF16) for _ in range(2)]; bQTf = [Buf(), Buf()]
        KTf = [AR.alloc([4, BLK], BF16) for _ in range(2)]; bKTf = [Buf(), Buf()]
        KTOKf = [AR.alloc([4, 512], BF16) for _ in range(2)]; bKTOKf = [Buf(), Buf()]
        VVs = [AR.alloc([4, D], BF16) for _ in range(2)]; bVVs = [Buf(), Buf()]
        ELFs = [AR.alloc([4, 4], F32) for _ in range(2)]; bELFs = [Buf(), Buf()]
        SGt = [AR.alloc([BLK], BF16) for _ in range(2)]; bSGt = [Buf(), Buf()]
        OT = AR.alloc([8, 128], F32); bOT = Buf()
        PKT = bank(7).bitcast(BF16)
        lnscale = math.log(128.0 ** -0.5)

        def merge2(*lists):
            lists = [l for l in lists if l]
            items = []
            for li_, l in enumerate(lists):
                for i, t in enumerate(l):
                    items.append(((i + 0.5) / len(l), li_, i, t))
            items.sort(key=lambda x: (x[0], x[1], x[2]))
            return [t for _, _, _, t in items]

        def P_thunks(bi):
            th = []
            sl = bi % 2
            src, nt = blk_src(bi)
            N = nt * 128
            v = 1 if bi == 0 else 0
            QT = (QTf[sl], QTb); bQT = (bQTf[sl], bQTb)
            KT = (KTf[sl], KTb); bKT = (bKTf[sl], bKTb)
            KTOK = (KTOKf[sl], KTOKb); bKTOK = (bKTOKf[sl], bKTOKb)
            VV = VVs[sl]; bVV = bVVs[sl]
            ELF = ELFs[sl]; bELF = bELFs[sl]

            def f_front():
                front_end(XT, bXT, HT, bHT, nt, 0, v, JK, bJK, (0, 1))
                if bi + 1 < NBLK:
                    srcn, ntn = blk_src(bi + 1)
                    S.dma("sp", XT[:, 0:ntn, :], srcn, writes=[bXT])
            th.append(f_front)

            def f_lr():
                for d in range(2):
                    for dc in range(8):
                        MM(bank(d)[0:16, 0:N], WQ[:, dc, 3072 + 16 * d:3088 + 16 * d], HT[:, dc, 0:N], dc == 0, dc == 7,
                           [bWQ, bHT], [psb[d]])
                    CP("dve", LR16[0:16, d, 0:N], bank(d)[0:16, 0:N], [psb[d]], [bLR])
            th.append(f_lr)
            th_head = th
            th = []
            for hp in range(2):
                for d in range(2):
                    def f_gate(hp=hp, d=d):
                        for h2 in range(2):
                            hd = 2 * hp + h2
                            pb = h2
                            MM(bank(pb)[:, 0:N], WG[0:16, d, hd * 128:(hd + 1) * 128], LR16[0:16, d, 0:N], True, True,
                               [bWG, bLR], [psb[pb]])
                            ACT(EB[d][:, h2, 0:N], bank(pb)[:, 0:N], AF.Exp, [psb[pb], bBGN], [bEB[d]],
                                bias=BGN[:, d, hd:hd + 1], scale=-1.0)
                        ACT(EB[d][:, :, 0:N], EB[d][:, :, 0:N], AF.Ln, [bEB[d]], [bEB[d]], bias=1.0)
                        for h2 in range(2):
                            if d == 0:
                                SCAN(CB[d][:, h2, 0:N], CM[:, 0, 0:N], EB[d][:, h2, 0:N], 0.0, [bCM, bEB[d]], [bCB[d]])
                            else:
                                SCAN(CB[d][:, h2, 0:N][:, ::-1], CM[:, 1, 0:N][:, ::-1], EB[d][:, h2, 0:N][:, ::-1], 0.0,
                                     [bCM, bEB[d]], [bCB[d]])
                        if d == 0:
                            ACT(ELF[:, 2 * hp:2 * hp + 2, 0:nt], CB[d][:, :, 127:N:128], AF.Exp, [bCB[d]], [bELF], scale=-1.0 / 16)
                        else:
                            ACT(ELB[:, 2 * hp:2 * hp + 2, 4 * bi:4 * bi + nt], CB[d][:, :, 0:N:128], AF.Exp, [bCB[d]], [bELB],
                                scale=-1.0 / 16)
                        ACT(EB[d][:, :, 0:N], CB[d][:, :, 0:N], AF.Exp, [bCB[d]], [bEB[d]], bias=lnscale, scale=-1.0 / 16)
                        ACT(CB[d][:, :, 0:N], CB[d][:, :, 0:N], AF.Exp, [bCB[d]], [bCB[d]], scale=1.0 / 16)
                    th.append(f_gate)
                for isk in range(2):
                    for h2 in range(2):
                        def f_qk(hp=hp, isk=isk, h2=h2):
                            hd = 2 * hp + h2
                            pb = h2
                            for dc in range(8):
                                MM(bank(pb)[:, 0:N], WQ[:, dc, 512 * isk + hd * 128:512 * isk + (hd + 1) * 128], HT[:, dc, 0:N],
                                   dc == 0, dc == 7, [bWQ, bHT], [psb[pb]])
                            for d in range(2):
                                if isk == 0:
                                    TT("dve", QT[d][:, hd, 0:N], bank(pb)[:, 0:N], EB[d][:, h2, 0:N], ALU.mult,
                                       [psb[pb], bEB[d]], [bQT[d]])
                                else:
                                    TT("dve", KT[d][:, hd, 0:N], bank(pb)[:, 0:N], CB[d][:, h2, 0:N], ALU.mult,
                                       [psb[pb], bCB[d]], [bKT[d]])
                        th.append(f_qk)
            th_chain = th
            th = []
            for vc in range(8):
                def f_g(vc=vc):
                    pb = vc % 2
                    for dc in range(8):
                        MM(bank(pb)[:, 0:N], WQ[:, dc, 2048 + vc * 128:2048 + (vc + 1) * 128], HT[:, dc, 0:N], dc == 0, dc == 7,
                           [bWQ, bHT], [psb[pb]])
                    ACT(SGt[vc % 2][:, 0:N], bank(pb)[:, 0:N], AF.Silu, [psb[pb]], [bSGt[vc % 2]])
                    S.dma("sp", sg_s[bi][:, vc * BLK:vc * BLK + N], SGt[vc % 2][:, 0:N], reads=[bSGt[vc % 2]], writes=[bsc[4][bi]])
                th.append(f_g)
            for tt in range(nt):
                for hf in range(2):
                    def f_v(tt=tt, hf=hf):
                        pb = hf
                        for dc in range(8):
                            MM(bank(pb), HT[:, dc, tt * 128:(tt + 1) * 128], WQ[:, dc, 1024 + hf * 512:1024 + (hf + 1) * 512],
                               dc == 0, dc == 7, [bWQ, bHT], [psb[pb]])
                        CP("act", VV[:, tt, hf * 512:(hf + 1) * 512], bank(pb), [psb[pb]], [bVV])
                    th.append(f_v)
            th_proj = th
            th = th_head + th_chain + th_proj
            for d in range(2):
                for tt in range(nt):
                    def f_kt(d=d, tt=tt):
                        for hd in range(4):
                            TR(PKT[:, hd * 128:(hd + 1) * 128], KT[d][:, hd, tt * 128:(tt + 1) * 128], ID16,
                               [bKT[d], bID16], [psb[7]], hd == 3)
                        CP("dve", KTOK[d][:, tt, :], PKT[:, 0:512], [psb[7]], [bKTOK[d]])
                    th.append(f_kt)

            def f_store():
                S.dma("sp", qtb_s[bi].rearrange("p (a b) -> p a b", a=4)[:, :, 0:N], QTb[:, :, 0:N], reads=[bQTb], writes=[bsc[0][bi]])
                S.dma("sp", ktb_s[bi].rearrange("p (a b) -> p a b", a=4)[:, :, 0:N], KTb[:, :, 0:N], reads=[bKTb], writes=[bsc[1][bi]])
                S.dma("sp", ktok_s[bi].rearrange("p (a b) -> p a b", a=4)[:, 0:nt, :], KTOKb[:, 0:nt, :], reads=[bKTOKb], writes=[bsc[2][bi]])
                S.dma("sp", vv_s[bi].rearrange("p (a b) -> p a b", a=4)[:, 0:nt, :], VV[:, 0:nt, :], reads=[bVV], writes=[bsc[3][bi]])
            th.append(f_store)
            return th

        def Q_thunks(bi):
            th = []
            sl = bi % 2
            src, nt = blk_src(bi)
            def pot_of(ci):
                def pot_out():
                    CP("act", OT, PS[:, 3 * 512:5 * 512].rearrange("p (a b) -> p a b", a=8), [psb[3], psb[4]], [bOT])
                    S.dma("sp", of_s[bi][:, ci * 1024:(ci + 1) * 1024], OT.rearrange("p a b -> p (a b)"),
                          reads=[bOT], writes=[bsc[5][bi]])
                return pot_out
            return scan_thunks(list(range(nt)), QTf[sl], KTf[sl], KTOKf[sl], VVs[sl],
                               [bQTf[sl], bKTf[sl], bKTOKf[sl], bVVs[sl]], 0, ELFs[sl], bELFs[sl], lambda ci: ci, pot_of)

        MEMSET("dve", S32, 0.0, [bS32])
        MEMSET("pool", S16, 0.0, [bS16])
        src0, nt0 = blk_src(0)
        S.dma("sp", XT[:, 0:nt0, :], src0, writes=[bXT])
        for t_ in P_thunks(0):
            t_()
        for bi in range(NBLK):
            pn = P_thunks(bi + 1) if bi + 1 < NBLK else []
            for t_ in merge2(pn, Q_thunks(bi)):
                t_()

        S.barrier()
        AR.reset(L0P)

        LQ = [AR.alloc([4, BLK], BF16) for _ in range(2)]
        LK = [AR.alloc([4, BLK], BF16) for _ in range(2)]
        LKT = [AR.alloc([4, 512], BF16) for _ in range(2)]
        LV = [AR.alloc([4, D], BF16) for _ in range(2)]
        LSG = [AR.alloc([8, BLK], BF16) for _ in range(2)]
        LOF = [AR.alloc([4, 8, 128], F32) for _ in range(2)]
        LX = [AR.alloc([4, D], F32) for _ in range(2)]
        bL = [[Buf() for _ in range(7)] for _ in range(2)]
        bLOFc = [[Buf() for _ in range(4)] for _ in range(2)]
        OSQ = AR.alloc([4, 8, 128], BF16); bOSQ = Buf()
        RSTD = AR.alloc([4, 4, 128], F32); bRSTD = Buf()
        OG = AR.alloc([4, 8, 128], BF16); bOG = Buf()
        TMPY = AR.alloc([D], F32); bTMPY = Buf()

        def loadA(bi, slot):
            src, nt = blk_src(bi)
            N = nt * 128
            S.dma("sp", LQ[slot][:, :, 0:N], qtb_s[bi].rearrange("p (a b) -> p a b", a=4)[:, :, 0:N], reads=[bsc[0][bi]], writes=[bL[slot][0]])
            S.dma("sp", LK[slot][:, :, 0:N], ktb_s[bi].rearrange("p (a b) -> p a b", a=4)[:, :, 0:N], reads=[bsc[1][bi]], writes=[bL[slot][1]])
            S.dma("sp", LKT[slot][:, 0:nt, :], ktok_s[bi].rearrange("p (a b) -> p a b", a=4)[:, 0:nt, :], reads=[bsc[2][bi]], writes=[bL[slot][2]])
            S.dma("sp", LV[slot][:, 0:nt, :], vv_s[bi].rearrange("p (a b) -> p a b", a=4)[:, 0:nt, :], reads=[bsc[3][bi]], writes=[bL[slot][3]])

        def loadL(bi, slot):
            src, nt = blk_src(bi)
            N = nt * 128
            for ci in range(nt - 1, -1, -1):
                S.dma("sp", LOF[slot][:, ci].rearrange("p a b -> p (a b)"), of_s[bi][:, ci * 1024:(ci + 1) * 1024],
                      reads=[bsc[5][bi]], writes=[bLOFc[slot][ci]])
            S.dma("sp", LSG[slot][:, :, 0:N], sg_s[bi].rearrange("p (a b) -> p a b", a=8)[:, :, 0:N], reads=[bsc[4][bi]], writes=[bL[slot][4]])
            S.dma("sp", LX[slot][:, 0:nt, :], src, writes=[bL[slot][6]])

        def A_thunks(bi, slot):
            th = []
            src, nt = blk_src(bi)
            bq, bk, bkt, bv = bL[slot][0:4]
            def pot_of(ci):
                def pot_out():
                    TT("dve", LOF[slot][:, ci], PS[:, 3 * 512:5 * 512].rearrange("p (a b) -> p a b", a=8),
                       LOF[slot][:, ci], ALU.add, [psb[3], psb[4], bLOFc[slot][ci]], [bLOFc[slot][ci]])
                return pot_out
            return scan_thunks(list(range(nt - 1, -1, -1)), LQ[slot], LK[slot], LKT[slot], LV[slot], [bq, bk, bkt, bv], 1,
                               ELB, bELB, lambda ci: 4 * bi + ci, pot_of)

        def B_thunks(bi, slot):
            th = []
            src, nt = blk_src(bi)
            N = nt * 128
            v = 1 if bi == 0 else 0
            bsg = bL[slot][4]; bx = bL[slot][6]
            bofs = bLOFc[slot][0:nt]
            l_norm = []; l_scale = []; l_out = []
            for ci in range(nt):
                def f_norm(ci=ci):
                    ACT(OSQ[:, ci], LOF[slot][:, ci], AF.Square, [bLOFc[slot][ci]], [bOSQ])
                    for hd in range(4):
                        for vv in range(2):
                            MM(bank(7)[:, hd * 128:(hd + 1) * 128], ONES16, OSQ[:, ci, hd * 2 + vv, :], vv == 0, vv == 1,
                               [bONES, bOSQ], [psb[7]], sig=(vv == 1 and hd == 3))
                    ACT(RSTD[:, ci], bank(7).rearrange("p (a b) -> p a b", a=4), AF.Ln, [psb[7]], [bRSTD], bias=EPS, scale=1.0 / 256)
                    ACT(RSTD[:, ci], RSTD[:, ci], AF.Exp, [bRSTD], [bRSTD], scale=-0.5)
                l_norm.append(f_norm)

            for ci in range(nt):
                def f_scale(ci=ci):
                    O5 = LOF[slot].rearrange("p c (h v) t -> p c h v t", v=2)
                    for vv in range(2):
                        STT(O5[:, ci, :, vv, :], O5[:, ci, :, vv, :], GN[:, vv:vv + 1], RSTD[:, ci], ALU.mult, ALU.mult,
                            [bLOFc[slot][ci], bGN, bRSTD], [bLOFc[slot][ci]])
                    TT("dve", OG[:, ci], LOF[slot][:, ci], LSG[slot][:, :, ci * 128:(ci + 1) * 128], ALU.mult,
                       [bLOFc[slot][ci], bsg], [bOG])
                l_scale.append(f_scale)
            for ci in range(nt):
                def f_out(ci=ci):
                    for hf in range(2):
                        for vc in range(8):
                            MM(bank(hf), OG[:, ci, vc, :], WOUT[:, vc, hf * 512:(hf + 1) * 512], vc == 0, vc == 7,
                               [bOG, bWOUT], [psb[hf]])
                    back_end(PS[:, 0:1024], [psb[0], psb[1]], LX[slot][:, ci, :], bx, GPL[0][v], bGPL[0][v], TMPY, bTMPY, JK, bJK)
                l_out.append(f_out)
            for ci in range(nt):
                th.append(l_norm[ci])
                if ci >= 1:
                    th.append(l_scale[ci - 1])
                    th.append(l_out[ci - 1])
            th.append(l_scale[nt - 1])
            th.append(l_out[nt - 1])

            def f_store():
                if bi == 0:
                    S.dma("sp", ctx1_d.rearrange("(tt p) d -> p tt d", p=128), LX[slot][:, 0:2, :], reads=[bx], writes=[bx1[0]])
                else:
                    s_ = (bi - 1) * BLK
                    S.dma("sp", x1_d[s_:s_ + BLK, :].rearrange("(tt p) d -> p tt d", p=128), LX[slot][:, 0:4, :], reads=[bx], writes=[bx1[bi]])
            th.append(f_store)
            return th

        MEMSET("dve", S32, 0.0, [bS32])
        MEMSET("pool", S16, 0.0, [bS16])
        orderB = [0] + list(range(NBLK - 1, 0, -1))
        loadA(orderB[0], 0)
        loadL(orderB[0], 0)
        nO = len(orderB)
        for oi in range(nO + 1):
            ath = A_thunks(orderB[oi], oi % 2) if oi < nO else []
            bth = B_thunks(orderB[oi - 1], (oi - 1) % 2) if oi >= 1 else []
            if oi + 1 < nO:
                loadA(orderB[oi + 1], (oi + 1) % 2)
            if oi == 0 and nO > 1:
                loadL(orderB[1], 1)
            for t_ in merge2(ath, bth):
                t_()
            if oi >= 1 and oi + 1 < nO:
                loadL(orderB[oi + 1], (oi + 1) % 2)
        S.barrier()
        AR.reset(L0)

    def _lru():
        zc_s = dscr("zc_s", [NB, 3, 128, 4 * BLK], F32)
        hf_s = dscr("hf_s", [NB, NCT, 128, BLK], F32)
        sgl_s = dscr("sgl_s", [NB, 3, 128, 4 * BLK], BF16)
        zcx_s = dscr("zcx_s", [128, NCT * CTX], F32); bzcx_s = Buf()
        bzs = [[[Buf() for _ in range(NCT)] for _ in range(NB)] for _ in range(3)]
        x1c = x1_d.rearrange("(r c) d -> r c d", c=64)
        outc = out_d.rearrange("(r c) d -> r c d", c=64)
        NGROUPS = ([0, 1], [2, 3], [4])

        L1 = AR.mark()
        LVT = AR.alloc([7, NCT], F32); bLVT = Buf()
        HBI = AR.alloc([2, 2, NCT], F32); bHBI = Buf()
        C8 = AR.alloc([2, 2, NCT], F32); bC8 = Buf()
        HC = AR.alloc([2, NCT], F32); bHC = Buf()
        JK = AR.alloc([D], BF16); bJK = Buf()
        RAs = [AR.alloc([4, BLK], F32) for _ in range(2)]; bRA = [Buf(), Buf()]
        QQs = [AR.alloc([4, BLK], F32) for _ in range(2)]; bQQ = [Buf(), Buf()]
        TIs = [AR.alloc([4, BLK], F32) for _ in range(2)]; bTI = [Buf(), Buf()]
        ZC16s = [AR.alloc([4, BLK], BF16) for _ in range(2)]; bZC16 = [[Buf() for _ in range(4)] for _ in range(2)]
        HFT = [AR.alloc([BLK], F32) for _ in range(2)]; bHFT = [Buf(), Buf()]
        WGA = [AR.alloc([2, 5, 256], BF16) for _ in range(2)]; bWGAs = [[Buf() for _ in range(5)] for _ in range(2)]
        S.dma("sp", LVT, lvT_d, writes=[bLVT])
        for d in range(2):
            for g in range(2):
                TS("dve", HBI[:, d, g, :], LVT[:, 1 + 3 * d + g, :], 0.5, ALU.mult, [bLVT], [bHBI])
            ACT(C8[:, d, 1, :], LVT[:, 3 + 3 * d, :], AF.Exp, [bLVT], [bC8], scale=-1.0)
            ACT(C8[:, d, 1, :], C8[:, d, 1, :], AF.Ln, [bC8], [bC8], bias=1.0)
            TS("dve", C8[:, d, 0, :], C8[:, d, 1, :], -4.0, ALU.mult, [bC8], [bC8])
            TS("dve", C8[:, d, 1, :], C8[:, d, 1, :], -8.0, ALU.mult, [bC8], [bC8])
        MEMSET("dve", HC, 0.0, [bHC])

        def load_gates(d):
            for g in range(2):
                for n in range(5):
                    S.dma("pool", WGA[g][:, :, n, :], lgate_d[2 * d + g, n].rearrange("(dt p) e -> p dt e", p=128),
                          writes=[bWGAs[g][n]])

        gcount = [0]

        def gate_parts(ns, N, d, z32, z16, rev, consume, slot):
            cos = [2 * n + et for n in ns for et in range(2)]
            RA = RAs[slot]; QQ = QQs[slot]; TI = TIs[slot]
            p1 = []
            p2 = []

            def mk1(gi, co):
                def f():
                    n = co // 2
                    et = co % 2
                    for g in range(2):
                        pb = 4 + g
                        for dt in range(2):
                            a16, b16 = z16(2 * n + dt)
                            MM(bank(pb)[:, 0:N], WGA[g][:, dt, n, et * 128:(et + 1) * 128], a16, dt == 0, dt == 1,
                               [bWGAs[g][n], b16], [psb[pb]])
                        dst, bdst = (RA, bRA[slot]) if g == 0 else (TI, bTI[slot])
                        ACT(dst[:, gi, 0:N], bank(pb)[:, 0:N], AF.Tanh, [psb[pb], bHBI], [bdst],
                            bias=HBI[:, d, g, co:co + 1], scale=0.5)
                    ACT(QQ[:, gi, 0:N], RA[:, gi, 0:N], AF.Exp, [bRA[slot], bC8], [bQQ[slot]],
                        bias=C8[:, d, 1, co:co + 1], scale=C8[:, d, 1, co:co + 1])
                    ACT(RA[:, gi, 0:N], RA[:, gi, 0:N], AF.Exp, [bRA[slot], bC8], [bRA[slot]],
                        bias=C8[:, d, 0, co:co + 1], scale=C8[:, d, 0, co:co + 1])
                return f

            def mksq():
                def f():
                    ng = len(cos)
                    ACT(QQ[:, 0:ng, 0:N], QQ[:, 0:ng, 0:N], AF.Sqrt, [bQQ[slot]], [bQQ[slot]], bias=1.0, scale=-1.0)
                return f

            def mk2(gi, co):
                def f():
                    a32, b32 = z32(co)
                    STT(TI[:, gi, 0:N], TI[:, gi, 0:N], 1.0, a32, ALU.add, ALU.mult, [bTI[slot], b32], [bTI[slot]])
                    STT(TI[:, gi, 0:N], QQ[:, gi, 0:N], 0.5, TI[:, gi, 0:N], ALU.mult, ALU.mult,
                        [bTI[slot], bQQ[slot]], [bTI[slot]])
                    hb = HFT[co % 2]; bhb = bHFT[co % 2]
                    if not rev:
                        SCAN(hb[:, 0:N], RA[:, gi, 0:N], TI[:, gi, 0:N], HC[:, d, co:co + 1],
                             [bRA[slot], bTI[slot], bHC], [bhb])
                        CP("dve", HC[:, d, co:co + 1], hb[:, N - 1:N], [bhb], [bHC])
                    else:
                        SCAN(hb[:, 0:N][:, ::-1], RA[:, gi, 0:N][:, ::-1], TI[:, gi, 0:N][:, ::-1], HC[:, d, co:co + 1],
                             [bRA[slot], bTI[slot], bHC], [bhb])
                        CP("dve", HC[:, d, co:co + 1], hb[:, 0:1], [bhb], [bHC])
                    if consume is not None:
                        consume[0](co, hb, bhb)
                return f

            def mk2b(gi, co):
                def f():
                    consume[1](co)
                return f
            for gi, co in enumerate(cos):
                p1.append(mk1(gi, co))
            p1.append(mksq())
            two = consume is not None and consume[1] is not None
            for gi, co in enumerate(cos):
                p2.append(mk2(gi, co))
                if two and gi >= 1:
                    p2.append(mk2b(gi - 1, cos[gi - 1]))
            if two:
                p2.append(mk2b(len(cos) - 1, cos[-1]))
            return p1, p2

        def merge(*lists):
            lists = [l for l in lists if l]
            if not lists:
                return []
            tot = max(len(l) for l in lists)
            items = []
            for li_, l in enumerate(lists):
                for i, t in enumerate(l):
                    items.append(((i + 0.5) / len(l), li_, i, t))
            items.sort(key=lambda x: (x[0], x[1], x[2]))
            return [t for _, _, _, t in items]

        def run(thunks):
            for t in thunks:
                t()

        L1P = AR.mark()

        WIN = AR.alloc([8, 2 * LW], BF16); bWINs = [Buf() for _ in range(8)]
        DIAG = AR.alloc([NCT, 4, 128], BF16); bDIAG = Buf()
        CW = AR.alloc([NCT, 4], F32); bCW = Buf()
        XT = AR.alloc([4, D], F32); bXT = Buf()
        HT = AR.alloc([8, BLK], BF16); bHT = Buf()
        ZH = [AR.alloc([NCT, 516], BF16) for _ in range(3)]
        bZH = [[Buf() for _ in range(NCT)] for _ in range(3)]
        TG = [AR.alloc([BLK], F32) for _ in range(2)]; bTG = [Buf(), Buf()]
        SGT1 = AR.alloc([4, BLK], BF16); bSGT1 = Buf()
        SGT = [SGT1, SGT1]; bSGT = [bSGT1, bSGT1]
        ZC32s = [AR.alloc([4, BLK], F32) for _ in range(2)]; bZC32 = [[Buf() for _ in range(4)] for _ in range(2)]
        for dc in range(8):
            S.dma("pool", WIN[:, dc, :], lwin_d[dc * 128:(dc + 1) * 128, :], writes=[bWINs[dc]])
        bWOUTs = [Buf() for _ in range(NCT)]
        WOUT_E = WIN.rearrange("p a b -> p (a b)")[:, 0:NCT * D].rearrange("p (a b) -> p a b", a=NCT)
        load_gates(0)
        S.dma("sp", CW, lcwT_d, writes=[bCW])
        for ct in range(NCT):
            for j in range(4):
                TS("dve", DIAG[:, ct, j, :], ID32, CW[:, ct, j:j + 1], ALU.mult, [bID32, bCW], [bDIAG])

        def stage1_z(nt, v, zslot, zprev, first, next_load):
            N = nt * 128
            zh = ZH[zslot]
            th = []

            def f_front():
                front_end(XT, bXT, HT, bHT, nt, 1, v, JK, bJK, (0, 1))
                if next_load is not None:
                    next_load()
            th.append(f_front)
            for ct in range(NCT):
                def f_z(ct=ct):
                    pb = ct % 2
                    for dc in range(8):
                        MM(bank(pb)[:, 0:N], WIN[:, dc, ct * 128:(ct + 1) * 128], HT[:, dc, 0:N], dc == 0, dc == 7,
                           [bWINs[dc], bHT], [psb[pb]])
                    CP("act", zh[:, ct, 2:2 + N], bank(pb)[:, 0:N], [psb[pb]], [bZH[zslot][ct]])
                th.append(f_z)

            def f_halo():
                if first:
                    MEMSET("pool", zh[:, :, 0:2], 0.0, bZH[zslot])
                else:
                    zp = ZH[zprev]
                    CP("pool", zh[:, :, 0:2], zp[:, :, 512:514], bZH[zprev], bZH[zslot])
                    CP("pool", zp[:, :, 514:515], zh[:, :, 2:3], bZH[zslot], bZH[zprev])
            th.append(f_halo)
            return th

        def stage1_g(k):
            th = []
            for gidx, ns in enumerate(NGROUPS):
                for gi0, n in enumerate(ns):
                    for dt in range(2):
                        def f(gidx=gidx, ns=ns, gi0=gi0, n=n, dt=dt):
                            ct = 2 * n + dt
                            gi = 2 * gi0 + dt
                            sg = SGT[gidx % 2]; bsg = bSGT[gidx % 2]
                            pb = 2 + dt
                            for dc in range(8):
                                MM(bank(pb), WIN[:, dc, LW + ct * 128:LW + (ct + 1) * 128], HT[:, dc, :], dc == 0, dc == 7,
                                   [bWINs[dc], bHT], [psb[pb]])
                            ACT(TG[dt], bank(pb), AF.Tanh, [psb[pb]], [bTG[dt]], scale=0.5)
                            STT(sg[:, gi, :], TG[dt], 1.0, bank(pb), ALU.add, ALU.mult, [bTG[dt], psb[pb]], [bsg])
                            if gi == 2 * len(ns) - 1:
                                S.dma("sp", sgl_s[k, gidx].rearrange("p (a b) -> p a b", a=4)[:, 0:2 * len(ns), :],
                                      sg[:, 0:2 * len(ns), :], reads=[bsg], writes=[bzs[2][k][gidx]])
                        th.append(f)
            return th

        def conv_thunks(zslot, N, ns, slot, store_blk, gidx, zcx):
            th = []
            for gi0, n in enumerate(ns):
                for dt in range(2):
                    def f(gi0=gi0, n=n, dt=dt):
                        ct = 2 * n + dt
                        gi = 2 * gi0 + dt
                        pb = 6 + dt
                        zh = ZH[zslot]
                        for j in range(4):
                            MM(bank(pb)[:, 0:N], DIAG[:, ct, j, :], zh[:, ct, j:j + N], j == 0, j == 3,
                               [bDIAG, bZH[zslot][ct]], [psb[pb]])
                        TS("dve", ZC32s[slot][:, gi, 0:N], bank(pb)[:, 0:N], LVT[:, 0, ct:ct + 1], ALU.add,
                           [psb[pb], bLVT], [bZC32[slot][gi]])
                        TS("dve", ZC16s[slot][:, gi, 0:N], bank(pb)[:, 0:N], LVT[:, 0, ct:ct + 1], ALU.add,
                           [psb[pb], bLVT], [bZC16[slot][gi]])
                        last = (gi == 2 * len(ns) - 1)
                        if last and store_blk is not None:
                            S.dma("sp", zc_s[store_blk, gidx].rearrange("p (a b) -> p a b", a=4)[:, 0:2 * len(ns), :],
                                  ZC32s[slot][:, 0:2 * len(ns), :], reads=bZC32[slot][0:2 * len(ns)], writes=[bzs[0][store_blk][gidx]])
                        if zcx:
                            S.dma("sp", zcx_s[:, ct * CTX:(ct + 1) * CTX], ZC32s[slot][:, gi, 0:N], reads=[bZC32[slot][gi]],
                                  writes=[bzcx_s])
                    th.append(f)
            return th

        pend = [[]]

        def do_group(zslot, N, gidx, store_blk, zcx, extra):
            ns = NGROUPS[gidx]
            slot = gcount[0] % 2
            gcount[0] += 1
            n0 = ns[0]
            z32 = lambda ct, slot=slot, n0=n0: (ZC32s[slot][:, ct - 2 * n0, 0:N], bZC32[slot][ct - 2 * n0])
            z16 = lambda ct, slot=slot, n0=n0: (ZC16s[slot][:, ct - 2 * n0, 0:N], bZC16[slot][ct - 2 * n0])
            if store_blk is not None:
                def consume0(co, hb, bhb):
                    S.dma("sp", hf_s[store_blk, co], hb, reads=[bhb], writes=[bzs[1][store_blk][co]])
                consume = (consume0, None)
            else:
                consume = None
            cv = conv_thunks(zslot, N, ns, slot, store_blk, gidx, zcx)
            p1, p2 = gate_parts(ns, N, 0, z32, z16, False, consume, slot)
            run(merge(cv + p1, pend[0], extra))
            pend[0] = p2

        S.dma("sp", XT[:, 0:2, :], ctx1_d.rearrange("(tt p) d -> p tt d", p=128), reads=[bx1[0]], writes=[bXT])
        MEMSET("pool", ZH[2], 0.0, bZH[2])

        def ld(k):
            def f():
                S.dma("sp", XT, x1c[:, 4 * k:4 * k + 4, :], writes=[bXT])
            return f
        def A_list(k):
            return stage1_z(4, 0, k % 3, (k - 1) % 3, k == 0, ld(k + 1) if k + 1 < NB else None) + stage1_g(k)
        run(stage1_z(2, 1, 2, 0, True, ld(0)))
        a0 = A_list(0)
        per0 = (len(a0) + 2) // 3
        for gidx in range(3):
            do_group(2, CTX, gidx, None, True, a0[gidx * per0:(gidx + 1) * per0])
        run(pend[0]); pend[0] = []
        _cut(1)
        for k in range(NB):
            al = A_list(k + 1) if k + 1 < NB else []
            if k + 1 == NB - 1:
                def f_wout():
                    for ct in range(NCT):
                        S.dma("pool", WOUT_E[:, ct, :], lwout_d[ct * 128:(ct + 1) * 128, :],
                              writes=[bWOUTs[ct]] + (bWINs if ct == 0 else []))
                al = al + [f_wout]
            if k >= 1:
                per = (len(al) + 2) // 3
                for gidx in range(3):
                    do_group((k - 1) % 3, BLK, gidx, k - 1, False, al[gidx * per:(gidx + 1) * per])
            else:
                run(al)
        MEMSET("pool", ZH[(NB - 1) % 3][:, :, 514:515], 0.0, bZH[(NB - 1) % 3])
        for gidx in range(3):
            do_group((NB - 1) % 3, BLK, gidx, NB - 1, False, [])
        run(pend[0]); pend[0] = []
        S.barrier()
        AR.reset(L1P)
        _cut(3)

        WOUT = AR.alloc([NCT, D], BF16)
        load_gates(1)
        GP1 = AR.alloc([D], F32); bGP1 = Buf()
        S.dma("sp", GP1, gpl_s[1][0], reads=[bgpl_s[1][0]], writes=[bGP1])
        LX = AR.alloc([4, D], F32); bLX = Buf()
        LZC = [AR.alloc([4, BLK], F32) for _ in range(3)]; bLZC = [Buf(), Buf(), Buf()]
        LHF = [AR.alloc([4, BLK], F32) for _ in range(3)]; bLHF = [Buf(), Buf(), Buf()]
        LSG = [AR.alloc([4, BLK], BF16) for _ in range(3)]; bLSG = [Buf(), Buf(), Buf()]
        SUM = [AR.alloc([BLK], F32) for _ in range(2)]; bSUM = [Buf(), Buf()]
        HG0 = AR.alloc([NCT, BLK], BF16)
        TMPY = AR.alloc([D], F32); bTMPY = Buf()
        mk_ = AR.mark()
        ZCX = AR.alloc([NCT, CTX], F32); bZCX = Buf()
        AR.reset(mk_)
        HG1 = AR.alloc([NCT, BLK], BF16)
        HGs = [HG0, HG1]; bHGs = [Buf(), bZCX]
        S.dma("sp", ZCX.rearrange("p a b -> p (a b)"), zcx_s, reads=[bzcx_s], writes=[bZCX])

        for gidx, ns in enumerate(NGROUPS):
            n0 = ns[0]
            slot = gcount[0] % 2
            gcount[0] += 1
            for gi, ct in enumerate([2 * n + dt for n in ns for dt in range(2)]):
                CP("dve", ZC16s[slot][:, gi, 0:CTX], ZCX[:, ct, :], [bZCX], [bZC16[slot][gi]])
            p1, p2 = gate_parts(ns, CTX, 1, lambda ct: (ZCX[:, ct, :], bZCX),
                                lambda ct, n0=n0, slot=slot: (ZC16s[slot][:, ct - 2 * n0, 0:CTX], bZC16[slot][ct - 2 * n0]), True, None, slot)
            run(p1); run(p2)

        seq = [(k, gidx) for k in range(NB - 1, -1, -1) for gidx in range(3)]
        _cut(4)

        def loadG(si):
            k, gidx = seq[si]
            slot = si % 3
            ng = 2 * len(NGROUPS[gidx])
            S.dma("sp", LZC[slot][:, 0:ng, :], zc_s[k, gidx].rearrange("p (a b) -> p a b", a=4)[:, 0:ng, :],
                  reads=[bzs[0][k][gidx]], writes=[bLZC[slot]])
            S.dma("sp", LSG[slot][:, 0:ng, :], sgl_s[k, gidx].rearrange("p (a b) -> p a b", a=4)[:, 0:ng, :],
                  reads=[bzs[2][k][gidx]], writes=[bLSG[slot]])
            for gi in range(ng):
                co = 2 * NGROUPS[gidx][0] + gi
                S.dma("sp", LHF[slot][:, gi, :], hf_s[k, co], reads=[bzs[1][k][co]], writes=[bLHF[slot]])

        def outproj_thunks(k):
            th = []
            HG = HGs[k % 2]; bHG = bHGs[k % 2]
            for tt in range(4):
                def f(tt=tt):
                    b0 = 0 if tt % 2 == 0 else 2
                    for hf in range(2):
                        for ct in range(NCT):
                            MM(bank(b0 + hf), HG[:, ct, tt * 128:(tt + 1) * 128], WOUT[:, ct, hf * 512:(hf + 1) * 512],
                               ct == 0, ct == NCT - 1, [bHG, bWOUTs[ct]], [psb[b0 + hf]])
                    back_end(PS[:, b0 * 512:b0 * 512 + 1024], [psb[b0], psb[b0 + 1]], LX[:, tt, :], bLX, GP1, bGP1,
                             TMPY, bTMPY, JK, bJK, li=1)
                    if tt == 3:
                        S.dma("sp", outc[:, 4 * k:4 * k + 4, :], LX, reads=[bLX], writes=[Buf()])
                        if k > 0:
                            S.dma("sp", LX, x1c[:, 4 * (k - 1):4 * (k - 1) + 4, :], writes=[bLX])
                th.append(f)
            return th

        loadG(0)
        S.dma("sp", LX, x1c[:, 4 * (NB - 1):4 * (NB - 1) + 4, :], writes=[bLX])
        pend2 = []
        opq = []
        for si, (k, gidx) in enumerate(seq):
            slot = si % 3
            ns = NGROUPS[gidx]
            n0 = ns[0]
            gslot = gcount[0] % 2
            gcount[0] += 1
            if si == 4:
                _cut(5)
            if si + 1 < len(seq):
                loadG(si + 1)
            cast = []
            for gi in range(2 * len(ns)):
                def fc(gi=gi, slot=slot, gslot=gslot):
                    CP("act", ZC16s[gslot][:, gi, :], LZC[slot][:, gi, :], [bLZC[slot]], [bZC16[gslot][gi]])
                cast.append(fc)

            def consume_a(co, hb, bhb, slot=slot, n0=n0):
                gi = co - 2 * n0
                sm = SUM[co % 2]; bsm = bSUM[co % 2]
                TT("dve", sm, hb, LHF[slot][:, gi, :], ALU.add, [bhb, bLHF[slot]], [bsm])

            def consume_b(co, slot=slot, n0=n0, k=k):
                gi = co - 2 * n0
                sm = SUM[co % 2]; bsm = bSUM[co % 2]
                STT(HGs[k % 2][:, co, :], sm, 0.5, LSG[slot][:, gi, :], ALU.mult, ALU.mult, [bsm, bLSG[slot]], [bHGs[k % 2]])
            consume = (consume_a, consume_b)
            p1, p2 = gate_parts(ns, BLK, 1, lambda ct, slot=slot, n0=n0: (LZC[slot][:, ct - 2 * n0, :], bLZC[slot]),
                                lambda ct, n0=n0, gslot=gslot: (ZC16s[gslot][:, ct - 2 * n0, :], bZC16[gslot][ct - 2 * n0]), True, consume, gslot)
            ex_now = []
            if opq and gidx >= 1:
                n_take = 2 if gidx == 1 else len(opq)
                ex_now = opq[:n_take]
                opq = opq[n_take:]
            run(merge(cast + p1, pend2, ex_now))
            pend2 = p2
            if gidx == 0 and si > 0:
                opq = outproj_thunks(k + 1)
        run(pend2)
        run(outproj_thunks(0))
        AR.reset(L1)

    if do1:
        try:
            _lru()
        except _Cut:
            pass

    S.barrier()
    if needed is not None:
        S.emit()
    return nc, S.used


def _colT(vec, n):
    return np.ascontiguousarray(np.asarray(vec, np.float32).reshape(n, 128).T)


def _consts():
    ident = np.eye(128, dtype=np.float32)
    cm = np.ones((128, 2, BLK), np.float32)
    cm[:, 0, 0::128] = 0.0
    cm[:, 1, 127::128] = 0.0
    j = np.arange(128)[:, None]
    i = np.arange(128)[None, :]
    tm = np.zeros((128, 2, 4, 128), np.float32)
    tm[:, 0, :, :] = (j <= i).astype(np.float32)[:, None, :]
    tm[:, 1, :, :] = (j >= i).astype(np.float32)[:, None, :]
    return ident, cm, tm


def _common_inputs(b, inp):
    ident, cm, tm = _consts()
    cvec = np.stack([_colT(inp["c"][b], 8), _colT(inp["c_ctx"], 8)], axis=-1)
    ada_bT = np.stack([_colT(inp["ada_b"][i], 24) for i in range(2)], axis=1)
    npreT = np.stack([_colT(inp["norm_pre"][i], 8) for i in range(2)], axis=1)
    return {
        "cvec": np.ascontiguousarray(cvec), "ada_w": inp["ada_w"], "ada_bT": np.ascontiguousarray(ada_bT),
        "ada_b": inp["ada_b"], "npreT": np.ascontiguousarray(npreT), "norm_post": inp["norm_post"],
        "ident": ident, "cmask": cm, "tmask": tm,
    }


def _gla_inputs(inp):
    bgT = np.stack([_colT(inp["gla_bg_f"][0], 4), _colT(inp["gla_bg_b"][0], 4)], axis=1)
    return {
        "gla_w_in": inp["gla_w_in"][0],
        "gla_wg": np.ascontiguousarray(np.stack([inp["gla_wg_f"][0], inp["gla_wg_b"][0]], axis=0)),
        "gla_bgT": np.ascontiguousarray(bgT),
        "gla_normT": _colT(inp["gla_norm"][0], 2),
        "gla_w_out": inp["gla_w_out"][0],
    }


def _lru_inputs(inp):
    vT = np.stack([_colT(inp[k][0], NCT) for k in
                   ("lru_conv_b", "lru_ba_f", "lru_bx_f", "lru_lam_f", "lru_ba_b", "lru_bx_b", "lru_lam_b")], axis=1)
    cwT = np.stack([_colT(inp["lru_conv_w"][0, j], NCT) for j in range(4)], axis=-1)
    gates = np.stack([inp["lru_wa_f"][0], inp["lru_wx_f"][0], inp["lru_wa_b"][0], inp["lru_wx_b"][0]], axis=0)
    return {
        "lru_w_in": inp["lru_w_in"][0], "lru_cwT": np.ascontiguousarray(cwT), "lru_vT": np.ascontiguousarray(vT),
        "lru_gates": np.ascontiguousarray(gates), "lru_w_out": inp["lru_w_out"][0],
    }


_NC_CACHE = {}


def _get_nc(layers):
    key = tuple(layers)
    if key not in _NC_CACHE:
        _NC_CACHE[key] = build(layers)
    return _NC_CACHE[key]


def kernel(**inputs):
    inp = {k: np.asarray(v, np.float32) for k, v in inputs.items()}
    ncores = 8
    maps = []
    for core in range(ncores):
        b = core // 2
        m = {"x": np.ascontiguousarray(inp["x"][b]), "ctx": np.ascontiguousarray(inp["ctx"][b])}
        m.update(_common_inputs(b, inp))
        m.update(_gla_inputs(inp))
        m.update(_lru_inputs(inp))
        maps.append(m)
    nc = _get_nc((0, 1))
    res = run_bass_kernel_spmd(nc, maps, core_ids=list(range(ncores)))
    out = np.stack([res.results[2 * b]["out"] for b in range(4)], axis=0)
    return out.astype(np.float32)
```
